# Optimizing a Trainium2 kernel written in Bass

```python
import jax, jax.numpy as jnp
from jax import lax
import numpy as np

D_MODEL = 4096
BATCH = 4
SEQ = 4096
DEPTH = 2

HEAD_DIM = 128
N_HEADS = D_MODEL // HEAD_DIM
H_REC = N_HEADS // 2
H_ATT = N_HEADS - H_REC
DK_REC = 128
DV_REC = HEAD_DIM
W_REC = H_REC * DV_REC
W_ATT = H_ATT * HEAD_DIM
MIX_WIDTH = W_REC + W_ATT
D_FF = 4 * D_MODEL
CHUNK = 64
Q_BLOCK = 128
N_MOD = 6
EPS = 1e-6
MASK_VALUE = -1e30
K_MAX = 1.0 - 1e-6

O_QR = H_REC * DK_REC
O_FR = O_QR + H_REC * DK_REC
O_IR = O_FR + W_REC
O_GR = O_IR + W_REC
O_QA = O_GR + W_ATT
O_KA = O_QA + W_ATT
O_VA = O_KA + W_ATT
IN_COLS = O_VA + H_ATT

kernel_name = "hymba_hgrn2_fox_adaln_block"


def _rms_norm(x, w):
    xf = x.astype(jnp.float32)
    y = xf * lax.rsqrt(jnp.mean(xf * xf, axis=-1, keepdims=True) + EPS)
    return (y * w.astype(jnp.float32)).astype(x.dtype)


def _hgrn2(q, f_logit, i, lb):
    B, S = q.shape[0], q.shape[1]
    nc = S // CHUNK
    z = f_logit.astype(jnp.float32)
    lbh = lb.astype(jnp.float32).reshape(H_REC, DK_REC)
    k = (1.0 - lbh) * jax.nn.sigmoid(-z)
    log_f = jnp.log1p(-jnp.minimum(k, K_MAX))
    qf = jax.nn.silu(q.astype(jnp.float32))
    vf = i.astype(jnp.float32)

    def to_chunks(a):
        return a.reshape(B, nc, CHUNK, H_REC, a.shape[-1]).transpose(1, 0, 3, 2, 4)

    mask = jnp.tril(jnp.ones((CHUNK, CHUNK), dtype=bool))[:, :, None]

    def step(state, inp):
        qc, kc, vc, lfc = inp
        G = jnp.cumsum(lfc, axis=2)
        o_inter = jnp.einsum('bhck,bhkv->bhcv', qc * jnp.exp(G), state)
        diff = G[:, :, :, None, :] - G[:, :, None, :, :]
        decay = jnp.where(mask, jnp.exp(jnp.minimum(diff, 0.0)), 0.0)
        A = jnp.einsum('bhtk,bhsk,bhtsk->bhts', qc, kc, decay)
        o_intra = jnp.einsum('bhts,bhsv->bhtv', A, vc)
        G_last = G[:, :, -1:, :]
        new_state = (jnp.exp(G_last[:, :, 0, :])[..., None] * state
                     + jnp.einsum('bhsk,bhsv->bhkv', kc * jnp.exp(G_last - G), vc))
        return new_state, o_inter + o_intra

    s0 = jnp.zeros((B, H_REC, DK_REC, DV_REC), jnp.float32)
    _, o = lax.scan(step, s0, (to_chunks(qf), to_chunks(k), to_chunks(vf), to_chunks(log_f)))
    return o.transpose(1, 0, 3, 2, 4).reshape(B, S, H_REC, DV_REC)


def _fox(q, k, v, fg_logit, fg_bias, qn_w, kn_w):
    B, S = q.shape[0], q.shape[1]
    q = _rms_norm(q, qn_w)
    k = _rms_norm(k, kn_w)
    log_f = jax.nn.log_sigmoid(fg_logit.astype(jnp.float32) + fg_bias.astype(jnp.float32))
    F = jnp.cumsum(log_f, axis=1).transpose(0, 2, 1)
    scale = HEAD_DIM ** -0.5
    outs = []
    for blk in range(S // Q_BLOCK):
        q0, q1 = blk * Q_BLOCK, (blk + 1) * Q_BLOCK
        s = jnp.einsum('bqhd,bkhd->bhqk', q[:, q0:q1], k[:, :q1]).astype(jnp.float32) * scale
        s = s + F[:, :, q0:q1, None] - F[:, :, None, :q1]
        causal = jnp.arange(q0, q1)[:, None] >= jnp.arange(q1)[None, :]
        p = jax.nn.softmax(jnp.where(causal, s, MASK_VALUE), axis=-1).astype(v.dtype)
        outs.append(jnp.einsum('bhqk,bkhd->bqhd', p, v[:, :q1]))
    return jnp.concatenate(outs, axis=1).reshape(B, S, W_ATT)


def setup_inputs(seed: int = 0) -> dict:
    key = jax.random.key(seed)
    ks = jax.random.split(key, 16)
    f32 = jnp.float32
    n = lambda k, shape, s: jax.random.normal(k, shape, f32) * s
    return {
        "x": n(ks[0], (BATCH, SEQ, D_MODEL), 1.0),
        "c": n(ks[1], (BATCH, D_MODEL), 1.0),
        "lower_bounds": n(ks[2], (DEPTH, H_REC * DK_REC), 0.5),
        "w_ada": n(ks[3], (DEPTH, D_MODEL, N_MOD * D_MODEL), 0.5 * D_MODEL ** -0.5),
        "b_ada": n(ks[4], (DEPTH, N_MOD * D_MODEL), 0.02),
        "norm_mix_w": 1.0 + n(ks[5], (DEPTH, D_MODEL), 0.02),
        "norm_ffn_w": 1.0 + n(ks[6], (DEPTH, D_MODEL), 0.02),
        "w_in": n(ks[7], (DEPTH, D_MODEL, IN_COLS), D_MODEL ** -0.5),
        "rec_norm_w": 1.0 + n(ks[8], (DEPTH, DV_REC), 0.02),
        "fg_bias": 1.0 + n(ks[9], (DEPTH, H_ATT), 0.5),
        "q_norm_w": 1.0 + n(ks[10], (DEPTH, HEAD_DIM), 0.02),
        "k_norm_w": 1.0 + n(ks[11], (DEPTH, HEAD_DIM), 0.02),
        "w_out": n(ks[12], (DEPTH, MIX_WIDTH, D_MODEL), MIX_WIDTH ** -0.5),
        "w_up": n(ks[13], (DEPTH, D_MODEL, D_FF), D_MODEL ** -0.5),
        "w_down": n(ks[14], (DEPTH, D_FF, D_MODEL), D_FF ** -0.5),
    }


def reference(x, c, lower_bounds, w_ada, b_ada, norm_mix_w, norm_ffn_w, w_in, rec_norm_w,
              fg_bias, q_norm_w, k_norm_w, w_out, w_up, w_down):
    B, S = x.shape[0], x.shape[1]
    p = jax.nn.softmax(lower_bounds.astype(jnp.float32), axis=0)
    lb_all = jnp.cumsum(p, axis=0) - p[0:1]
    c_act = jax.nn.silu(c)
    for l in range(DEPTH):
        mod = (c_act @ w_ada[l] + b_ada[l]).reshape(B, N_MOD, 1, D_MODEL)
        shift1, scale1, gate1 = mod[:, 0], mod[:, 1], mod[:, 2]
        shift2, scale2, gate2 = mod[:, 3], mod[:, 4], mod[:, 5]

        h = _rms_norm(x, norm_mix_w[l]) * (1.0 + scale1) + shift1
        proj = h @ w_in[l]
        q_r, f_r, i_r, g_r, q_a, k_a, v_a, fg_a = jnp.split(
            proj, [O_QR, O_FR, O_IR, O_GR, O_QA, O_KA, O_VA], axis=-1)

        o_rec = _hgrn2(q_r.reshape(B, S, H_REC, DK_REC), f_r.reshape(B, S, H_REC, DK_REC),
                       i_r.reshape(B, S, H_REC, DV_REC), lb_all[l])
        o_rec = _rms_norm(o_rec, rec_norm_w[l]).reshape(B, S, W_REC).astype(x.dtype)
        o_rec = o_rec * jax.nn.silu(g_r)

        o_att = _fox(q_a.reshape(B, S, H_ATT, HEAD_DIM), k_a.reshape(B, S, H_ATT, HEAD_DIM),
                     v_a.reshape(B, S, H_ATT, HEAD_DIM), fg_a, fg_bias[l], q_norm_w[l], k_norm_w[l])

        mix = jnp.concatenate([o_rec, o_att], axis=-1) @ w_out[l]
        x = x + gate1 * mix

        h2 = _rms_norm(x, norm_ffn_w[l]) * (1.0 + scale2) + shift2
        x = x + gate2 * (jnp.square(jax.nn.relu(h2 @ w_up[l])) @ w_down[l])
    return x
```

```python
import math
import numpy as np
import concourse.bass as bass
import concourse.mybir as mybir
from concourse.bass_utils import run_bass_kernel_spmd

F32, BF16 = mybir.dt.float32, mybir.dt.bfloat16
AF = mybir.ActivationFunctionType
ALU = mybir.AluOpType
EPS = 1e-6
K_MAX = 1.0 - 1e-6
NCORES = 8
ARENA_BYTES = 204 * 1024


class Cfg:
    def __init__(self, D=4096, SEQ=4096, B=4, DFF=None):
        self.D, self.SEQ, self.B = D, SEQ, B
        self.NT = SEQ // 2
        self.KC = D // 128
        self.NH = D // 128
        self.HR = self.NH // 2
        self.HA = self.NH - self.HR
        self.WR, self.WA = self.HR * 128, self.HA * 128
        self.MIXW = self.WR + self.WA
        self.KM = self.MIXW // 128
        self.DFF = DFF or 4 * D
        self.FC = self.DFF // 128
        self.IC = 4 * self.WR + 3 * self.WA + self.HA
        self.WS = 4
        self.AGLIM = 4 * 1024 * 1024
        self.DR = D // 4
        self.TT = 512
        self.TF = 256
        self.NTT = self.NT // 512
        self.NB = self.NT // 128
        self.CH = 32
        self.NCH = self.NT // self.CH
        self.stop = None
        self.dump = None


class Op:
    __slots__ = ("eng", "fn", "deps", "signal", "sem", "cnt", "kind", "idx")


class Buf:
    REG = []

    def __init__(self):
        self.writers, self.readers = [], []
        Buf.REG.append(self)


class Tile(Buf):
    def __init__(self, ap):
        Buf.__init__(self)
        self.ap = ap


class Sched:
    def __init__(self):
        self.ops = []
        self.ring = {"sp": 24, "pool": 16}
        self.ring_i = {"sp": 0, "pool": 0}
        self.ring_last = {}
        self.last = {}
        self.stopped = False

    def op(self, eng, fn, reads=(), writes=(), appends=(), kind="c"):
        if self.stopped:
            return None
        o = Op()
        o.eng, o.fn, o.kind, o.signal, o.sem, o.cnt = eng, fn, kind, False, None, 0
        deps = []
        for b in reads:
            deps += b.writers
        for b in writes:
            deps += b.writers + b.readers
        for b in appends:
            deps += b.readers
        if kind == "d":
            slot = (eng, self.ring_i[eng] % self.ring[eng])
            self.ring_i[eng] += 1
            if slot in self.ring_last:
                deps.append(self.ring_last[slot])
            self.ring_last[slot] = o
            o.sem = slot
            o.signal = True
        elif kind == "cc":
            o.sem = ("cc", 0)
            o.signal = True
        else:
            o.sem = (eng, -1)
        best = {}
        for d in deps:
            if d.eng == "pe" and eng == "pe" and d.kind == "c" and kind == "c":
                continue
            p = best.get(d.sem)
            if p is None or p.idx < d.idx:
                best[d.sem] = d
        dl = list(best.values())
        for d in dl:
            d.signal = True
        o.deps = dl
        o.idx = len(self.ops)

        def add(lst, o):
            lst[:] = [x for x in lst if x.sem != o.sem]
            lst.append(o)

        for b in reads:
            add(b.readers, o)
        for b in writes:
            b.writers = [o]
            b.readers = []
        for b in appends:
            add(b.writers, o)
        self.ops.append(o)
        if kind == "c" or kind == "cc":
            self.last[o.sem] = o
        return o

    def barrier(self):
        if self.stopped:
            return
        tails = list(self.last.values()) + list(self.ring_last.values())
        for e in ("sp", "pe", "act", "dve", "pool"):
            o = Op()
            o.eng, o.fn, o.kind, o.signal, o.sem, o.cnt = e, None, "b", False, None, 0
            o.idx = len(self.ops)
            o.deps = list(tails)
            for d in tails:
                d.signal = True
            self.ops.append(o)
        for b in Buf.REG:
            b.writers, b.readers = [], []

    def emit(self, nc, stack):
        sems = {}

        def getsem(key):
            if key not in sems:
                sems[key] = stack.enter_context(nc.semaphore("s_%s_%d" % (key[0], key[1] + 1)))
            return sems[key]

        counts = {}
        for o in self.ops:
            if o.signal:
                inc = 16 if o.kind == "d" else 1
                counts[o.sem] = counts.get(o.sem, 0) + inc
                o.cnt = counts[o.sem]
                getsem(o.sem)
        block = stack.enter_context(nc.Block())
        engmap = {"sp": block.sync, "pe": block.tensor, "act": block.scalar, "dve": block.vector,
                  "pool": block.gpsimd}
        ops = self.ops
        for ename, deco in engmap.items():
            def run(e, ename=ename):
                waited = {}
                for o in ops:
                    if o.eng != ename:
                        continue
                    need = {}
                    for d in o.deps:
                        if need.get(d.sem, 0) < d.cnt:
                            need[d.sem] = d.cnt
                    for sk, c in need.items():
                        if waited.get(sk, 0) < c:
                            e.wait_ge(sems[sk], c)
                            waited[sk] = c
                    if o.fn is not None:
                        ins = o.fn(e)
                        if o.signal:
                            if o.kind == "d":
                                ins.then_inc(sems[o.sem], 16)
                            elif o.kind == "cc":
                                ins.then_inc(sems[o.sem])
                            else:
                                ins.then_inc(sems[o.sem], 1)
            deco(run)


class Arena:
    def __init__(self, big):
        self.big = big
        self.base = 0
        self.ptr = 0

    def alloc(self, cols, dt, parts=128):
        nb = cols * (4 if dt == F32 else 2)
        nb = (nb + 63) // 64 * 64
        assert self.ptr + nb <= ARENA_BYTES, ("arena overflow", self.ptr, nb)
        a = self.big[:, self.ptr // 4:(self.ptr + nb) // 4]
        self.ptr += nb
        if dt != F32:
            a = a.bitcast(dt)
        a = a[0:parts, 0:cols]
        return Tile(a)

    def persist(self):
        self.base = self.ptr

    def reset(self):
        self.ptr = self.base


def build(cfg):
    Buf.REG = []
    from contextlib import ExitStack
    c = cfg
    D, NT, KC, HR, HA, WR, WA, DFF, FC, IC, DR = c.D, c.NT, c.KC, c.HR, c.HA, c.WR, c.WA, c.DFF, c.FC, c.IC, c.DR
    MIXW, KM, TT, NTT, NB, NCH = c.MIXW, c.KM, c.TT, c.NTT, c.NB, c.NCH
    WS = c.WS
    AGLIM = c.AGLIM

    def p2floor(v):
        p = 1
        while p * 2 <= v:
            p *= 2
        return p

    def wchunk(r, cc_):
        return min(r, p2floor(AGLIM // (WS * cc_ * 2)))

    RCK = min(c.WA, p2floor(AGLIM // (2 * c.NT * 2)))
    RCV = min(c.NT, max(128, p2floor(AGLIM // (2 * c.WA * 2))))
    GS = 128
    NCG = GS // c.CH
    TF = c.TF
    CH = c.CH
    NC5 = 512 // CH
    G4 = [[0, 1, 2, 3], [4, 5, 6, 7]]
    nc = bass.Bass("TRN2", target_bir_lowering=False)
    stack = ExitStack()

    def din(name, shape, dt=F32):
        return nc.dram_tensor(name, list(shape), dt, kind="ExternalInput").ap()

    def dscr(name, shape, dt):
        return nc.dram_tensor(name, list(shape), dt).ap()

    x_in = din("x", [NT, D])
    cT_in = din("cT", [DR, 4])
    wada_in = din("w_ada", [2, DR, 6 * D])
    bada_in = din("b_ada", [12 * KC, 128])
    nwm_in = din("norm_mix_w", [2 * KC, 128])
    nwf_in = din("norm_ffn_w", [2 * KC, 128])
    lb_in = din("lower_bounds", [2 * HR, 128])
    rnw_in = din("rec_norm_w", [2, 128])
    qnw_in = din("q_norm_w", [2, 128])
    knw_in = din("k_norm_w", [2, 128])
    fgb_in = din("fg_bias", [1, 2 * HA])
    win_in = din("w_in", [2, DR, IC])
    wout_in = din("w_out", [2, MIXW // WS, D])
    wup_in = din("w_up", [2, DR, DFF])
    wdn_in = din("w_down", [2, DFF // WS, D])
    consts_in = din("consts", [128, 384])
    selb_in = din("selb", [16, 16 * 128])
    sel_in = din("sel", [128, 8])
    y_out = nc.dram_tensor("y", [NT, D], F32, kind="ExternalOutput").ap()

    wsh = {}
    wfull = {}
    wspec = {"in": (win_in, DR, IC), "out": (wout_in, MIXW // WS, D), "up": (wup_in, DR, DFF),
             "dn": (wdn_in, DFF // WS, D)}
    for l in range(2):
        for k, (_, r, cc_) in wspec.items():
            wsh[(k, l)] = dscr("wsh_%s%d" % (k, l), [r, cc_], BF16)
            wfull[(k, l)] = dscr("wfull_%s%d" % (k, l), [WS * r, cc_], BF16)
    NM = 2 * 6 * KC * 4
    pm_d = dscr("pm_d", [128, NM], F32)
    pmg_d = dscr("pmg_d", [WS * 128, NM], F32)
    XT = [dscr("XT%d" % i, [D, NT], F32) for i in range(2)]
    QRT = dscr("QRT", [WR, NT], BF16)
    ZRT = dscr("ZRT", [WR, NT], F32)
    GRT = dscr("GRT", [WR, NT], BF16)
    IRd = dscr("IRd", [NT, WR], BF16)
    QAT = dscr("QAT", [WA, NT], F32)
    KAT = dscr("KAT", [WA, NT], F32)
    VAd = dscr("VAd", [NT, WA], BF16)
    VA2 = dscr("VA2", [2 * NT, WA], BF16)
    FGd = dscr("FGd", [NT, HA], F32)
    LFd = dscr("LFd", [NT, HA], F32)
    LF2 = dscr("LF2", [2 * NT, HA], F32)
    QNT = dscr("QNT", [WA, NT], BF16)
    KNT = dscr("KNT", [WA, NT], BF16)
    KN2 = dscr("KN2", [2 * WA, NT], BF16)
    QGT = dscr("QGT", [WR, NT], BF16)
    KGT = dscr("KGT", [WR, NT], BF16)
    SEND = dscr("SEND", [WR, 128], F32)
    SGAT = dscr("SGAT", [2 * WR, 128], F32)
    MIXT = dscr("MIXT", [MIXW, NT], BF16)

    big = stack.enter_context(nc.sbuf_tensor("big", [128, ARENA_BYTES // 4], F32))
    PS = [Tile(stack.enter_context(nc.psum_tensor("ps%d" % i, [128, 512], F32))[:]) for i in range(7)]
    PSB = Tile(stack.enter_context(nc.psum_tensor("psb", [128, 1024], BF16))[:])
    A = Arena(big)
    S = Sched()

    def dma(eng, out, in_, reads=(), writes=(), appends=(), **kw):
        return S.op(eng, lambda e: e.dma_start(out=out, in_=in_, **kw), reads, writes, appends, kind="d")

    def mm(out, lhsT, rhs, start, stop, reads=(), writes=(), appends=()):
        return S.op("pe", lambda e: e.matmul(out, lhsT, rhs, start=start, stop=stop), reads, writes, appends)

    def tr(out, in_, ident, reads=(), writes=(), appends=()):
        return S.op("pe", lambda e: e.transpose(out, in_, ident), reads, writes, appends)

    def act(out, in_, func, reads=(), writes=(), bias=None, scale=None, appends=()):
        kw = {}
        if bias is not None:
            kw["bias"] = bias
        if scale is not None:
            kw["scale"] = scale
        return S.op("act", lambda e: e.activation(out=out, in_=in_, func=func, **kw), reads, writes, appends)

    def ts(eng, out, in0, s1, s2, op0, op1=None, reads=(), writes=(), appends=()):
        if op1 is None:
            return S.op(eng, lambda e: e.tensor_scalar(out=out, in0=in0, scalar1=s1, scalar2=None, op0=op0),
                        reads, writes, appends)
        return S.op(eng, lambda e: e.tensor_scalar(out=out, in0=in0, scalar1=s1, scalar2=s2, op0=op0, op1=op1),
                    reads, writes, appends)

    def tt(eng, out, in0, in1, op, reads=(), writes=(), appends=()):
        return S.op(eng, lambda e: e.tensor_tensor(out=out, in0=in0, in1=in1, op=op), reads, writes, appends)

    def stt(out, in0, scalar, in1, op0, op1, reads=(), writes=(), appends=()):
        return S.op("dve", lambda e: e.scalar_tensor_tensor(out=out, in0=in0, scalar=scalar, in1=in1,
                                                            op0=op0, op1=op1), reads, writes, appends)

    def cp(eng, out, in_, reads=(), writes=(), appends=()):
        if eng == "act":
            return act(out, in_, AF.Identity, reads, writes, appends=appends)
        return S.op(eng, lambda e: e.tensor_copy(out=out, in_=in_), reads, writes, appends)

    def rsq(T):
        act(T.ap, T.ap, AF.Sqrt, reads=[T], writes=[T])
        S.op("dve", lambda e: e.reciprocal(out=T.ap, in_=T.ap), reads=[T], writes=[T])

    def phase():
        S.barrier()
        A.reset()

    def chk(name):
        if c.stop == name and not S.stopped:
            S.barrier()
            src = scr_names[c.dump]
            r, cc_ = src.shape
            r = min(r, NT)
            cc_ = min(cc_, D)
            dma("pool", y_out[0:r, 0:cc_], src[0:r, 0:cc_], max_dma_last_dim=2048)
            S.barrier()
            S.stopped = True

    scr_names = dict(XT0=XT[0], XT1=XT[1], QRT=QRT, ZRT=ZRT, GRT=GRT, IRd=IRd, QAT=QAT, KAT=KAT, VAd=VAd, VA2=VA2, FGd=FGd,
                     LFd=LFd, LF2=LF2, QNT=QNT, KNT=KNT, KN2=KN2, QGT=QGT, KGT=KGT, SEND=SEND, SGAT=SGAT, MIXT=MIXT,
                     pm_d=pm_d, pmg_d=pmg_d, wfull_in0=wfull[("in", 0)])

    consts = A.alloc(384, F32)
    ident, tri, ones = consts.ap[:, 0:128], consts.ap[:, 128:256], consts.ap[:, 256:384]
    cb16 = A.alloc(384, BF16)
    identb, trib, onesb = cb16.ap[:, 0:128], cb16.ap[:, 128:256], cb16.ap[:, 256:384]
    selb = A.alloc(16 * 128, F32, parts=16)
    sel = A.alloc(8, F32)
    modT = A.alloc(12 * KC, F32)
    gsc = A.alloc(4 * KC, F32)
    nwT = A.alloc(4 * KC, F32)
    lbT = A.alloc(2 * HR, F32)
    omlb = A.alloc(2 * HR, F32)
    rnwT = A.alloc(2, F32)
    qnwT = A.alloc(2, F32)
    knwT = A.alloc(2, F32)
    fgb = A.alloc(2 * HA, F32)
    Mb = A.alloc(4, F32)
    onesrow = A.alloc(64, F32)
    negm = A.alloc(128, F32)
    A.persist()

    def mod(l, j):
        return modT.ap[:, (l * 6 + j) * KC:(l * 6 + j + 1) * KC]

    dma("sp", consts.ap, consts_in, writes=[consts])
    dma("sp", selb.ap, selb_in, writes=[selb])
    dma("sp", sel.ap, sel_in, writes=[sel])
    cp("dve", cb16.ap, consts.ap, reads=[consts], writes=[cb16])
    dma("sp", fgb.ap, fgb_in[0, :].partition_broadcast(128), writes=[fgb])
    cp("dve", onesrow.ap, consts.ap[:, 256:320], reads=[consts], writes=[onesrow])
    ts("dve", negm.ap, consts.ap[:, 128:256], -1.0, 1e30, ALU.add, ALU.mult, reads=[consts], writes=[negm])

    def load_T(dst_tile, dst_ap, src, R):
        r0 = 0
        while r0 < R:
            r = min(128, R - r0)
            t = A.alloc(128, F32)
            dma("sp", t.ap[0:r, :], src[r0:r0 + r, :], writes=[t])
            tr(PS[0].ap[:, 0:r], t.ap[0:r, :], ident[0:r, 0:r], reads=[t, consts], writes=[PS[0]])
            cp("dve", dst_ap[:, r0:r0 + r], PS[0].ap[:, 0:r], reads=[PS[0]], appends=[dst_tile])
            r0 += r

    badaT = A.alloc(12 * KC, F32)
    load_T(badaT, badaT.ap, bada_in, 12 * KC)
    load_T(nwT, nwT.ap[:, 0:2 * KC], nwm_in, 2 * KC)
    load_T(nwT, nwT.ap[:, 2 * KC:4 * KC], nwf_in, 2 * KC)
    load_T(lbT, lbT.ap, lb_in, 2 * HR)
    load_T(rnwT, rnwT.ap, rnw_in, 2)
    load_T(qnwT, qnwT.ap, qnw_in, 2)
    load_T(knwT, knwT.ap, knw_in, 2)
    S.op("dve", lambda e: e.memset(omlb.ap[:, 0:HR], 1.0), writes=[omlb])
    dl = A.alloc(HR, F32)
    tt("dve", dl.ap, lbT.ap[:, 0:HR], lbT.ap[:, HR:2 * HR], ALU.subtract, reads=[lbT], writes=[dl])
    act(omlb.ap[:, HR:2 * HR], dl.ap, AF.Sigmoid, reads=[dl], appends=[omlb])
    for l in range(2):
        rq = A.alloc(128, F32, parts=1)
        rk = A.alloc(128, F32, parts=1)
        mq = A.alloc(4, F32, parts=1)
        dma("sp", rq.ap, qnw_in[l:l + 1, :], writes=[rq])
        dma("sp", rk.ap, knw_in[l:l + 1, :], writes=[rk])
        tt("dve", rq.ap, rq.ap, rq.ap, ALU.mult, reads=[rq], writes=[rq])
        tt("dve", rk.ap, rk.ap, rk.ap, ALU.mult, reads=[rk], writes=[rk])
        S.op("dve", lambda e, rq=rq, mq=mq: e.tensor_reduce(out=mq.ap[:, 0:1], in_=rq.ap, axis=mybir.AxisListType.X,
                                                         op=ALU.max), reads=[rq], writes=[mq])
        S.op("dve", lambda e, rk=rk, mq=mq: e.tensor_reduce(out=mq.ap[:, 1:2], in_=rk.ap, axis=mybir.AxisListType.X,
                                                         op=ALU.max), reads=[rk, mq], writes=[mq])
        tt("dve", mq.ap[:, 2:3], mq.ap[:, 0:1], mq.ap[:, 1:2], ALU.mult, reads=[mq], writes=[mq])
        tt("dve", mq.ap[:, 3:4], mq.ap[:, 0:1], mq.ap[:, 1:2], ALU.mult, reads=[mq], writes=[mq])
        act(mq.ap[:, 2:4], mq.ap[:, 2:4], AF.Sqrt, reads=[mq], writes=[mq])
        mm(PS[1].ap[:, 0:2], consts.ap[0:1, 256:384], mq.ap[:, 2:4], True, True,
           reads=[mq, consts], writes=[PS[1]])
        ts("dve", Mb.ap[:, 2 * l:2 * l + 1], PS[1].ap[:, 0:1], math.sqrt(128.0), None, ALU.mult, reads=[PS[1]],
           appends=[Mb])
        ts("dve", Mb.ap[:, 2 * l + 1:2 * l + 2], PS[1].ap[:, 0:1], -math.sqrt(128.0), None, ALU.mult, reads=[PS[1]],
           appends=[Mb])

    chk("p0")
    kchunks = (DR + 127) // 128
    kr = min(128, DR)
    scT = A.alloc(4 * kchunks, F32)
    for kc in range(kchunks):
        dma("sp", scT.ap[0:kr, kc * 4:(kc + 1) * 4], cT_in[kc * kr:(kc + 1) * kr, :], appends=[scT])
    scS = A.alloc(4 * kchunks, F32)
    act(scS.ap[0:kr, :], scT.ap[0:kr, :], AF.Silu, reads=[scT], writes=[scS])
    pmod = A.alloc(NM, F32)
    wa = [A.alloc(kchunks * 512, F32) for _ in range(2)]
    ncb = 6 * D // 512
    it = 0
    for l in range(2):
        for cb in range(ncb):
            w = wa[it % 2]
            dma("sp", w.ap[0:kr, :].rearrange("p (k n) -> p k n", k=kchunks),
                wada_in[l, :, cb * 512:(cb + 1) * 512].rearrange("(k p) n -> p k n", p=kr), writes=[w])
            ps = PS[2 + it % 2]
            for sb in range(4):
                for kc in range(kchunks):
                    mm(ps.ap[:, sb * 4:(sb + 1) * 4], w.ap[0:kr, kc * 512 + sb * 128:kc * 512 + (sb + 1) * 128],
                       scS.ap[0:kr, kc * 4:(kc + 1) * 4], kc == 0, kc == kchunks - 1, reads=[w, scS], writes=[ps])
            off = (l * ncb + cb) * 16
            cp("dve", pmod.ap[:, off:off + 16], ps.ap[:, 0:16], reads=[ps], appends=[pmod])
            it += 1
    pmb = Buf()
    dma("pool", pm_d, pmod.ap, reads=[pmod], writes=[pmb])
    chk("p1a")
    S.op("pool", lambda e: e.collective_compute("AllGather", ALU.bypass, replica_groups=G4,
                                                ins=[pm_d], outs=[pmg_d]), reads=[pmb], writes=[pmb], kind="cc")
    pg = A.alloc(WS * NM, F32)
    dma("pool", pg.ap.rearrange("p (r n) -> p r n", r=WS), pmg_d.rearrange("(r p) n -> p r n", p=128),
        reads=[pmb], writes=[pg])
    for r in range(1, WS):
        tt("dve", pg.ap[:, 0:NM], pg.ap[:, 0:NM], pg.ap[:, r * NM:(r + 1) * NM], ALU.add, reads=[pg], writes=[pg])
    pgv = pg.ap[:, 0:NM].rearrange("p (m b) -> p m b", b=4)
    cp("dve", modT.ap, badaT.ap, reads=[badaT], writes=[modT])
    for b in range(4):
        stt(modT.ap, pgv[:, :, b], sel.ap[:, b:b + 1], modT.ap, ALU.mult, ALU.add, reads=[pg, sel, modT],
            writes=[modT])
    for l in range(2):
        for sub in range(2):
            sc_ = mod(l, 3 * sub + 1)
            g = gsc.ap[:, (l * 2 + sub) * KC:(l * 2 + sub + 1) * KC]
            nw = nwT.ap[:, (sub * 2 + l) * KC:(sub * 2 + l + 1) * KC]
            stt(g, sc_, 1.0, nw, ALU.add, ALU.mult, reads=[modT, nwT], appends=[gsc])

    chk("p1")
    wb = {}
    for l in range(2):
        for k in ("in", "out", "up", "dn"):
            src, r, cc_ = wspec[k]
            b = Buf()
            wb[(k, l)] = b
            r0 = 0
            while r0 < r:
                rr = min(128, r - r0)
                dma("pool", wsh[(k, l)][r0:r0 + rr, :], src[l, r0:r0 + rr, :], appends=[b], max_dma_last_dim=8192)
                r0 += rr
            rc = wchunk(r, cc_)
            for j in range(r // rc):
                S.op("pool", lambda e, k=k, l=l, j=j, rc=rc: e.collective_compute(
                    "AllGather", ALU.bypass, replica_groups=G4, ins=[wsh[(k, l)][j * rc:(j + 1) * rc, :]],
                    outs=[wfull[(k, l)][WS * j * rc:(WS * j + WS) * rc, :]]),
                    reads=[b], appends=[b], kind="cc")

    chk("p2")
    phase()
    xt_b = Buf()
    xin = [A.alloc(D, F32) for _ in range(2)]
    xst = [A.alloc(KC * 128, F32) for _ in range(2)]
    for tb in range(NB):
        xi, xs = xin[tb % 2], xst[tb % 2]
        dma("sp", xi.ap, x_in[tb * 128:(tb + 1) * 128, :], writes=[xi])
        for k4 in range(0, KC, 4):
            ps = PS[(k4 // 4) % 4]
            for j in range(4):
                tr(ps.ap[:, j * 128:(j + 1) * 128], xi.ap[:, (k4 + j) * 128:(k4 + j + 1) * 128], ident,
                   reads=[xi, consts], writes=[ps])
            cp("act" if (k4 // 4) % 2 else "dve", xs.ap[:, k4 * 128:(k4 + 4) * 128], ps.ap, reads=[ps], appends=[xs])
        dma("pool", XT[0][:, tb * 128:(tb + 1) * 128].rearrange("(k p) t -> p k t", p=128),
            xs.ap.rearrange("p (k t) -> p k t", k=KC), reads=[xs], appends=[xt_b])

    def prenorm_tmps(tsz):
        return dict(sq=[A.alloc(tsz, F32) for _ in range(2)], rstd=A.alloc(tsz, F32),
                    tmp=[A.alloc(tsz, F32) for _ in range(2)])

    def prenorm(l, sub, XTin, t0, hT, xr, tsz, tm):
        g = gsc.ap[:, (l * 2 + sub) * KC:(l * 2 + sub + 1) * KC]
        sh = mod(l, 3 * sub)
        G4 = min(4, KC)
        pss = PS[6]
        sq, rstd, tmp = tm["sq"], tm["rstd"], tm["tmp"]
        for k4 in range(0, KC, G4):
            xb = xr[(k4 // G4) % 2]
            dma("sp", xb.ap.rearrange("p (k t) -> p k t", k=G4),
                XTin[k4 * 128:(k4 + G4) * 128, t0:t0 + tsz].rearrange("(k p) t -> p k t", p=128), writes=[xb])
            for j in range(G4):
                s_ = sq[j % 2]
                act(s_.ap, xb.ap[:, j * tsz:(j + 1) * tsz], AF.Square, reads=[xb], writes=[s_])
                mm(pss.ap[:, 0:tsz], ones, s_.ap, k4 + j == 0, k4 + j == KC - 1, reads=[s_, consts], writes=[pss])
        ts("dve", rstd.ap, pss.ap[:, 0:tsz], 1.0 / D, EPS, ALU.mult, ALU.add, reads=[pss], writes=[rstd])
        rsq(rstd)
        for k4 in range(0, KC, G4):
            xb = xr[(k4 // G4) % 2]
            dma("sp", xb.ap.rearrange("p (k t) -> p k t", k=G4),
                XTin[k4 * 128:(k4 + G4) * 128, t0:t0 + tsz].rearrange("(k p) t -> p k t", p=128), writes=[xb])
            for j in range(G4):
                kc = k4 + j
                t_ = tmp[j % 2]
                tt("dve", t_.ap, xb.ap[:, j * tsz:(j + 1) * tsz], rstd.ap, ALU.mult, reads=[xb, rstd], writes=[t_])
                act(hT.ap[:, kc * tsz:(kc + 1) * tsz], t_.ap, AF.Identity, reads=[t_, gsc, modT], appends=[hT],
                    scale=g[:, kc:kc + 1], bias=sh[:, kc:kc + 1])

    chk("pro")
    if True:
      for l in range(2):
          Xa = XT[0]
          Xb = XT[1]
          phase()
          Win = wfull[("in", l)]
          hT = A.alloc(KC * TT, BF16)
          xr = [A.alloc(min(4, KC) * TT, F32) for _ in range(2)]
          wt = [A.alloc(KC * 256, BF16) for _ in range(3)]
          ost = [A.alloc(TT, F32) for _ in range(4)]
          wfg = A.alloc(KC * HA, BF16)
          ptm = prenorm_tmps(TT)
          scr = Buf()
          O_QR, O_FR, O_IR, O_GR = 0, WR, 2 * WR, 3 * WR
          O_QA = 4 * WR
          O_KA, O_VA, O_FG = O_QA + WA, O_QA + 2 * WA, O_QA + 3 * WA
          dma("sp", wfg.ap.rearrange("p (k n) -> p k n", k=KC),
              Win[:, O_FG:O_FG + HA].rearrange("(k p) n -> p k n", p=128), reads=[wb[("in", l)]], writes=[wfg])
          wi = 0
          oi = 0
          for ttile in range(NTT):
              t0 = ttile * TT
              prenorm(l, 0, Xa, t0, hT, xr, TT, ptm)
              fm_groups = [(O_QR, WR, "q"), (O_FR, WR, "z"), (O_GR, WR, "g"), (O_QA, WA, "qa"), (O_KA, WA, "ka")]
              for (c0, width, kind) in fm_groups:
                  for cb in range(0, width, 256):
                      w = wt[wi % 3]
                      wi += 1
                      dma("sp", w.ap.rearrange("p (k n) -> p k n", k=KC),
                          Win[:, c0 + cb:c0 + cb + 256].rearrange("(k p) n -> p k n", p=128),
                          reads=[wb[("in", l)]], writes=[w])
                      for hb in range(2):
                          ps = PS[oi % 4]
                          for kc in range(KC):
                              mm(ps.ap, w.ap[:, kc * 256 + hb * 128:kc * 256 + (hb + 1) * 128],
                                 hT.ap[:, kc * TT:(kc + 1) * TT], kc == 0, kc == KC - 1, reads=[w, hT], writes=[ps])
                          o = ost[oi % 4]
                          oi += 1
                          row = cb + hb * 128
                          if kind == "q":
                              ob = o.ap.bitcast(BF16)[:, 0:TT]
                              act(ob, ps.ap, AF.Silu, reads=[ps], writes=[o])
                              dma("pool", QRT[row:row + 128, t0:t0 + TT], ob, reads=[o], appends=[scr])
                          elif kind == "g":
                              ob = o.ap.bitcast(BF16)[:, 0:TT]
                              act(ob, ps.ap, AF.Silu, reads=[ps], writes=[o])
                              dma("pool", GRT[row:row + 128, t0:t0 + TT], ob, reads=[o], appends=[scr])
                          else:
                              dst = {"z": ZRT, "qa": QAT, "ka": KAT}[kind]
                              cp("dve", o.ap, ps.ap, reads=[ps], writes=[o])
                              dma("pool", dst[row:row + 128, t0:t0 + TT], o.ap, reads=[o], appends=[scr])
              for (c0, width, dst) in [(O_IR, WR, IRd), (O_VA, WA, VAd)]:
                  for cb in range(0, width, 256):
                      w = wt[wi % 3]
                      wi += 1
                      dma("sp", w.ap.rearrange("p (k n) -> p k n", k=KC),
                          Win[:, c0 + cb:c0 + cb + 256].rearrange("(k p) n -> p k n", p=128),
                          reads=[wb[("in", l)]], writes=[w])
                      for tsb in range(TT // 128):
                          ps = PS[oi % 4]
                          for kc in range(KC):
                              mm(ps.ap[:, 0:256], hT.ap[:, kc * TT + tsb * 128:kc * TT + (tsb + 1) * 128],
                                 w.ap[:, kc * 256:(kc + 1) * 256], kc == 0, kc == KC - 1, reads=[w, hT], writes=[ps])
                          o = ost[oi % 4]
                          oi += 1
                          ob = o.ap.bitcast(BF16)[:, 0:256]
                          cp("act", ob, ps.ap[:, 0:256], reads=[ps], writes=[o])
                          dma("pool", dst[t0 + tsb * 128:t0 + (tsb + 1) * 128, cb:cb + 256], ob, reads=[o],
                              appends=[scr])
              for tsb in range(TT // 128):
                  ps = PS[oi % 4]
                  for kc in range(KC):
                      mm(ps.ap[:, 0:HA], hT.ap[:, kc * TT + tsb * 128:kc * TT + (tsb + 1) * 128],
                         wfg.ap[:, kc * HA:(kc + 1) * HA], kc == 0, kc == KC - 1, reads=[wfg, hT], writes=[ps])
                  o = ost[oi % 4]
                  oi += 1
                  cp("dve", o.ap[:, 0:HA], ps.ap[:, 0:HA], reads=[ps], writes=[o])
                  dma("pool", FGd[t0 + tsb * 128:t0 + (tsb + 1) * 128, :], o.ap[:, 0:HA], reads=[o], appends=[scr])

          chk("A%d" % l)
          phase()
          scale = 1.0 / math.sqrt(128.0)
          qin = [A.alloc(TT, F32) for _ in range(2)]
          sqb = [A.alloc(TT, F32) for _ in range(2)]
          rs = [A.alloc(TT, F32) for _ in range(2)]
          qo = [A.alloc(TT, BF16) for _ in range(2)]
          i = 0
          for (src, dst, wT) in [(QAT, QNT, qnwT), (KAT, KNT, knwT)]:
              for h in range(HA):
                  for ttile in range(NTT):
                      t0 = ttile * TT
                      q_, s_, r_, o_ = qin[i % 2], sqb[i % 2], rs[i % 2], qo[i % 2]
                      ps = PS[i % 2]
                      i += 1
                      dma("sp", q_.ap, src[h * 128:(h + 1) * 128, t0:t0 + TT], writes=[q_])
                      act(s_.ap, q_.ap, AF.Square, reads=[q_], writes=[s_])
                      mm(ps.ap, ones, s_.ap, True, True, reads=[s_, consts], writes=[ps])
                      ts("dve", r_.ap, ps.ap, 1.0 / 128, EPS, ALU.mult, ALU.add, reads=[ps], writes=[r_])
                      rsq(r_)
                      tt("dve", q_.ap, q_.ap, r_.ap, ALU.mult, reads=[q_, r_], writes=[q_])
                      act(o_.ap, q_.ap, AF.Identity, reads=[q_, wT], writes=[o_], scale=wT.ap[:, l:l + 1])
                      dma("pool", dst[h * 128:(h + 1) * 128, t0:t0 + TT], o_.ap, reads=[o_], appends=[scr])
          fg = A.alloc(NB * HA, F32)
          dma("sp", fg.ap.rearrange("p (b h) -> p b h", b=NB), FGd.rearrange("(b p) h -> p b h", p=128), writes=[fg])
          fgv = fg.ap.rearrange("p (b h) -> p b h", b=NB)
          for b in range(NB):
              tt("dve", fgv[:, b, :], fgv[:, b, :], fgb.ap[:, l * HA:(l + 1) * HA], ALU.add, reads=[fg, fgb],
                 writes=[fg])
          ab = A.alloc(NB * HA, F32)
          act(ab.ap, fg.ap, AF.Abs, reads=[fg], writes=[ab])
          act(ab.ap, ab.ap, AF.Exp, reads=[ab], writes=[ab], scale=-1.0)
          act(ab.ap, ab.ap, AF.Ln, reads=[ab], writes=[ab], bias=1.0)
          lfcat = A.alloc(2 * NB * HA, F32)
          lfo = lfcat.ap[:, NB * HA:2 * NB * HA]
          ts("dve", fg.ap, fg.ap, 0.0, None, ALU.min, reads=[fg], writes=[fg])
          tt("dve", lfo, fg.ap, ab.ap, ALU.subtract, reads=[fg, ab], writes=[lfcat])
          lfb = Buf()
          dma("pool", LFd.rearrange("(b p) h -> p b h", p=128), lfo.rearrange("p (b h) -> p b h", b=NB),
              reads=[lfcat], writes=[lfb])
          S.op("pool", lambda e: e.collective_compute("AllGather", ALU.bypass,
                                                      replica_groups=[[0, 1], [2, 3], [4, 5], [6, 7]],
                                                      ins=[LFd], outs=[LF2]), reads=[lfb], writes=[lfb], kind="cc")
          dma("pool", lfcat.ap[:, 0:NB * HA].rearrange("p (b h) -> p b h", b=NB),
              LF2[0:NT, :].rearrange("(b p) h -> p b h", p=128), reads=[lfb, lfcat], writes=[lfcat])
          ts("dve", lfcat.ap[:, 0:NB * HA], lfcat.ap[:, 0:NB * HA], sel.ap[:, 5:6], None, ALU.mult, reads=[lfcat, sel],
             writes=[lfcat])
          kvb = Buf()
          for j in range(WA // RCK):
              S.op("pool", lambda e, j=j: e.collective_compute(
                  "AllGather", ALU.bypass, replica_groups=[[0, 1], [2, 3], [4, 5], [6, 7]],
                  ins=[KNT[j * RCK:(j + 1) * RCK, :]], outs=[KN2[2 * j * RCK:(2 * j + 2) * RCK, :]]),
                  reads=[scr], appends=[kvb], kind="cc")
          for j in range(NT // RCV):
              S.op("pool", lambda e, j=j: e.collective_compute(
                  "AllGather", ALU.bypass, replica_groups=[[0, 1], [2, 3], [4, 5], [6, 7]],
                  ins=[VAd[j * RCV:(j + 1) * RCV, :]], outs=[VA2[2 * j * RCV:(2 * j + 2) * RCV, :]]),
                  reads=[scr], appends=[kvb], kind="cc")
          Fc = A.alloc(2 * NB * HA, F32)
          lfv = lfcat.ap.rearrange("p (b h) -> p b h", b=2 * NB)
          for b in range(2 * NB):
              ps = PS[2 + b % 2]
              for b2 in range(b):
                  mm(ps.ap[:, 0:HA], ones, lfv[:, b2, :], b2 == 0, False, reads=[lfcat, consts], writes=[ps])
              mm(ps.ap[:, 0:HA], tri, lfv[:, b, :], b == 0, True, reads=[lfcat, consts], writes=[ps])
              cp("dve", Fc.ap[:, b * HA:(b + 1) * HA], ps.ap[:, 0:HA], reads=[ps], appends=[Fc])
          nb_ = A.alloc(2 * NB * HA, F32)
          ts("dve", nb_.ap, Fc.ap, -1.0, Mb.ap[:, 2 * l + 1:2 * l + 2], ALU.mult, ALU.add, reads=[Fc, Mb], writes=[nb_])
          ts("dve", nb_.ap[:, 0:NB * HA], nb_.ap[:, 0:NB * HA], sel.ap[:, 4:5], None, ALU.add, reads=[nb_, sel],
             writes=[nb_])
          Frow = A.alloc(NT, F32, parts=16)
          for b in range(NB):
              ps = PS[4 + b % 2]
              tr(ps.ap[0:HA, 0:128], Fc.ap[:, (NB + b) * HA:(NB + b + 1) * HA], ident, reads=[Fc, consts], writes=[ps])
              cp("dve", Frow.ap[0:HA, b * 128:(b + 1) * 128], ps.ap[0:HA, 0:128], reads=[ps], appends=[Frow])

          chk("B%d" % l)
          S.barrier()
          kTp = [A.alloc(NT, BF16) for _ in range(2)]
          kTo = [A.alloc(NT, BF16) for _ in range(2)]
          vp = [A.alloc(NB * 128, BF16) for _ in range(2)]
          vo = [A.alloc(NB * 128, BF16) for _ in range(2)]
          qn = [A.alloc(TT, BF16) for _ in range(2)]
          Fqb = [A.alloc(TT, F32) for _ in range(2)]
          stmp = [A.alloc(TT, F32) for _ in range(2)]
          PT = [A.alloc(TT, BF16) for _ in range(3)]
          rinv = A.alloc(TT, F32)
          oat = [A.alloc(TT, BF16) for _ in range(2)]
          bi = 0
          qi = 0
          for h in range(HA):
              hb = h % 2
              kj, ko = (h * 128) // RCK, (h * 128) % RCK
              dma("sp", kTp[hb].ap, KN2[2 * kj * RCK + ko:2 * kj * RCK + ko + 128, :], reads=[kvb], writes=[kTp[hb]])
              dma("sp", kTo[hb].ap, KNT[h * 128:(h + 1) * 128, :], reads=[scr], writes=[kTo[hb]])
              for vj in range(NT // RCV):
                  q_ = RCV // 128
                  dma("sp", vp[hb].ap[:, vj * RCV:(vj + 1) * RCV].rearrange("p (b d) -> p b d", b=q_),
                      VA2[2 * vj * RCV:2 * vj * RCV + RCV, h * 128:(h + 1) * 128].rearrange("(b p) d -> p b d", p=128),
                      reads=[kvb], appends=[vp[hb]])
              dma("sp", vo[hb].ap.rearrange("p (b d) -> p b d", b=NB),
                  VAd[:, h * 128:(h + 1) * 128].rearrange("(b p) d -> p b d", p=128), reads=[scr], writes=[vo[hb]])
              for j in range(NTT):
                  t0 = j * TT
                  q_ = qn[qi % 2]
                  fq = Fqb[qi % 2]
                  psO, psR = PS[4 + qi % 2 * 0], PS[5]
                  psO = PS[4]
                  qi += 1
                  dma("sp", q_.ap, QNT[h * 128:(h + 1) * 128, t0:t0 + TT], reads=[scr], writes=[q_])
                  mm(PS[6].ap, selb.ap[0:HA, h * 128:(h + 1) * 128], Frow.ap[0:HA, t0:t0 + TT], True, True,
                     reads=[selb, Frow], writes=[PS[6]])
                  cp("dve", fq.ap, PS[6].ap, reads=[PS[6]], writes=[fq])
                  blocks = [(0, kb) for kb in range(NB)] + [(1, kb) for kb in range(4 * j + 4)]
                  for bidx, (own, kb) in enumerate(blocks):
                      last = bidx == len(blocks) - 1
                      kT = (kTo if own else kTp)[hb]
                      v = (vo if own else vp)[hb]
                      dI = kb - 4 * j if own else -1
                      c0 = 128 * dI if dI > 0 else 0
                      psS = PS[bi % 2]
                      st = stmp[bi % 2]
                      p_ = PT[bi % 3]
                      bi += 1
                      mm(psS.ap[:, c0:TT], kT.ap[:, kb * 128:(kb + 1) * 128], q_.ap[:, c0:TT], True, True,
                         reads=[kT, q_], writes=[psS])
                      tt("dve", st.ap[:, c0:TT], psS.ap[:, c0:TT], fq.ap[:, c0:TT], ALU.add, reads=[psS, fq], writes=[st])
                      if dI >= 0:
                          tt("dve", st.ap[:, c0:c0 + 128], st.ap[:, c0:c0 + 128], negm.ap, ALU.add, reads=[st, negm],
                             writes=[st])
                      col = ((NB if own else 0) + kb) * HA + h
                      act(p_.ap[:, c0:TT], st.ap[:, c0:TT], AF.Exp, reads=[st, nb_], writes=[p_], scale=scale,
                          bias=nb_.ap[:, col:col + 1])
                      mm(psO.ap[:, c0:TT], v.ap[:, kb * 128:(kb + 1) * 128], p_.ap[:, c0:TT], bidx == 0, last,
                         reads=[v, p_], writes=[psO])
                      mm(psR.ap[:, c0:TT], onesb, p_.ap[:, c0:TT], bidx == 0, last, reads=[p_, cb16], writes=[psR])
                  S.op("dve", lambda e, psR=psR: e.reciprocal(out=rinv.ap, in_=psR.ap), reads=[psR], writes=[rinv])
                  o_ = oat[qi % 2]
                  tt("dve", o_.ap, psO.ap, rinv.ap, ALU.mult, reads=[psO, rinv], writes=[o_])
                  dma("pool", MIXT[WR + h * 128:WR + (h + 1) * 128, t0:t0 + TT], o_.ap, reads=[o_], appends=[scr])

          chk("C%d" % l)
          phase()
          NG = NT // 512
          CS = A.alloc(HR * NCH * 3, F32)
          csv = CS.ap.rearrange("p (h c k) -> p h c k", h=HR, c=NCH)
          zt = [A.alloc(512, F32) for _ in range(2)]
          qs = [A.alloc(512, BF16) for _ in range(2)]
          kk = [A.alloc(512, F32) for _ in range(2)]
          lf = [A.alloc(512, F32) for _ in range(2)]
          Gt = [A.alloc(512, F32) for _ in range(2)]
          E1 = [A.alloc(512, F32) for _ in range(2)]
          qg = [A.alloc(512, BF16) for _ in range(2)]
          kg = [A.alloc(512, BF16) for _ in range(2)]
          dlrs = [A.alloc(NC5, F32) for _ in range(2)]
          i = 0
          for h in range(HR):
              for g in range(NG):
                  t0 = g * 512
                  z_, q_, k_, l_, G_, E_, qg_, kg_ = (zt[i % 2], qs[i % 2], kk[i % 2], lf[i % 2], Gt[i % 2], E1[i % 2],
                                                      qg[i % 2], kg[i % 2])
                  i += 1
                  dma("sp", z_.ap, ZRT[h * 128:(h + 1) * 128, t0:t0 + 512], reads=[scr], writes=[z_])
                  dma("sp", q_.ap, QRT[h * 128:(h + 1) * 128, t0:t0 + 512], reads=[scr], writes=[q_])
                  act(z_.ap, z_.ap, AF.Sigmoid, reads=[z_], writes=[z_], scale=-1.0)
                  ts("dve", k_.ap, z_.ap, omlb.ap[:, l * HR + h:l * HR + h + 1], K_MAX, ALU.mult, ALU.min,
                     reads=[z_, omlb], writes=[k_])
                  act(l_.ap, k_.ap, AF.Ln, reads=[k_], writes=[l_], scale=-1.0, bias=1.0)
                  for cchunk in range(NC5):
                      sl = slice(cchunk * CH, (cchunk + 1) * CH)
                      S.op("dve", lambda e, G_=G_, l_=l_, sl=sl: e.tensor_tensor_scan(
                          out=G_.ap[:, sl], data0=ones[:, 0:CH], data1=l_.ap[:, sl], initial=0.0, op0=ALU.mult,
                          op1=ALU.add), reads=[l_, consts], writes=[G_])
                  Gv = G_.ap.rearrange("p (c t) -> p c t", t=CH)
                  cidx = g * NC5
                  act(csv[:, h, cidx:cidx + NC5, 0], Gv[:, :, CH - 1], AF.Exp, reads=[G_], appends=[CS])
                  act(csv[:, h, cidx:cidx + NC5, 2], Gv[:, :, CH // 2 - 1], AF.Exp, reads=[G_], appends=[CS])
                  dlr = dlrs[i % 2]
                  tt("dve", dlr.ap, Gv[:, :, CH - 1], Gv[:, :, CH // 2 - 1], ALU.subtract, reads=[G_], writes=[dlr])
                  act(csv[:, h, cidx:cidx + NC5, 1], dlr.ap, AF.Exp, reads=[dlr], appends=[CS])
                  for cchunk in range(NC5):
                      sl = slice(cchunk * CH, (cchunk + 1) * CH)
                      ts("dve", l_.ap[:, sl], G_.ap[:, sl], Gv[:, cchunk, CH // 2 - 1:CH // 2], None, ALU.subtract, reads=[G_, l_],
                         writes=[l_])
                  act(E_.ap, l_.ap, AF.Exp, reads=[l_], writes=[E_])
                  tt("dve", qg_.ap, q_.ap, E_.ap, ALU.mult, reads=[q_, E_], writes=[qg_])
                  act(E_.ap, l_.ap, AF.Exp, reads=[l_, qg_], writes=[E_], scale=-1.0)
                  tt("dve", kg_.ap, k_.ap, E_.ap, ALU.mult, reads=[k_, E_], writes=[kg_])
                  dma("pool", QGT[h * 128:(h + 1) * 128, t0:t0 + 512], qg_.ap, reads=[qg_], appends=[scr])
                  dma("pool", KGT[h * 128:(h + 1) * 128, t0:t0 + 512], kg_.ap, reads=[kg_], appends=[scr])

          chk("D1%d" % l)
          S.barrier()
          SS = A.alloc(HR * 128, F32)
          S.op("dve", lambda e: e.memset(SS.ap, 0.0), writes=[SS])
          kgl = [A.alloc(HR * GS, BF16) for _ in range(2)]
          qgl = [A.alloc(HR * GS, BF16) for _ in range(2)]
          vl = [A.alloc(NCG * WR, BF16, parts=CH) for _ in range(2)]
          kgtm = [A.alloc(128, BF16, parts=CH) for _ in range(4)]
          utmp = [A.alloc(128, F32) for _ in range(4)]
          Sb = [A.alloc(128, BF16) for _ in range(4)]
          ATm = [A.alloc(CH, BF16, parts=CH) for _ in range(4)]
          OTs = A.alloc(HR * GS, F32)
          sqo = [A.alloc(GS, F32) for _ in range(2)]
          rso = [A.alloc(GS, F32) for _ in range(2)]
          grl = [A.alloc(GS, BF16) for _ in range(2)]
          yo = [A.alloc(GS, BF16) for _ in range(2)]
          sendb = Buf()
          ui = 0
          for pss_ in range(2):
              emit = pss_ == 1
              for g in range(NT // GS):
                  t0 = g * GS
                  kgl_, qgl_, vl_ = kgl[g % 2], qgl[g % 2], vl[g % 2]
                  dma("sp", kgl_.ap.rearrange("p (h t) -> p h t", h=HR),
                      KGT[:, t0:t0 + GS].rearrange("(h p) t -> p h t", p=128), reads=[scr], writes=[kgl_])
                  dma("sp", vl_.ap.rearrange("p (c w) -> p c w", c=NCG),
                      IRd[t0:t0 + GS, :].rearrange("(c p) w -> p c w", p=CH), reads=[scr], writes=[vl_])
                  if emit:
                      dma("sp", qgl_.ap.rearrange("p (h t) -> p h t", h=HR),
                          QGT[:, t0:t0 + GS].rearrange("(h p) t -> p h t", p=128), reads=[scr], writes=[qgl_])
                  for cchunk in range(NCG):
                      cg = g * NCG + cchunk
                      for h in range(HR):
                          u = ui % 4
                          ui += 1
                          kgc = kgl_.ap[:, h * GS + cchunk * CH:h * GS + (cchunk + 1) * CH]
                          vc = vl_.ap[:, cchunk * WR + h * 128:cchunk * WR + (h + 1) * 128]
                          Sh = SS.ap[:, h * 128:(h + 1) * 128]
                          if emit:
                              qgc = qgl_.ap[:, h * GS + cchunk * CH:h * GS + (cchunk + 1) * CH]
                              act(Sb[u].ap, Sh, AF.Identity, reads=[SS, CS], writes=[Sb[u]], scale=csv[:, h, cg, 2:3])
                              psA = PS[u % 2]
                              mm(psA.ap[0:CH, 0:CH], kgc, qgc, True, True, reads=[kgl_, qgl_], writes=[psA])
                              tt("dve", ATm[u].ap, psA.ap[0:CH, 0:CH], tri[0:CH, 0:CH], ALU.mult, reads=[psA, consts],
                                 writes=[ATm[u]])
                              psO = PS[2 + u % 2]
                              mm(psO.ap[:, 0:CH], vc, ATm[u].ap, True, False, reads=[vl_, ATm[u]], writes=[psO])
                              mm(psO.ap[:, 0:CH], Sb[u].ap, qgc, False, True, reads=[Sb[u], qgl_], writes=[psO])
                              cp("act", OTs.ap[:, h * GS + cchunk * CH:h * GS + (cchunk + 1) * CH], psO.ap[:, 0:CH],
                                 reads=[psO], appends=[OTs])
                          tr(PSB.ap[0:CH, (u % 2) * 128:(u % 2) * 128 + 128], kgc, identb, reads=[kgl_, cb16],
                             writes=[PSB])
                          cp("act", kgtm[u].ap, PSB.ap[0:CH, (u % 2) * 128:(u % 2) * 128 + 128], reads=[PSB],
                             writes=[kgtm[u]])
                          psU = PS[4 + u % 2]
                          mm(psU.ap[:, 0:128], kgtm[u].ap, vc, True, True, reads=[kgtm[u], vl_], writes=[psU])
                          ts("dve", utmp[u].ap, psU.ap[:, 0:128], csv[:, h, cg, 1:2], None, ALU.mult, reads=[psU, CS],
                             writes=[utmp[u]])
                          stt(Sh, Sh, csv[:, h, cg, 0:1], utmp[u].ap, ALU.mult, ALU.add, reads=[SS, utmp[u], CS],
                              writes=[SS])
                  if emit:
                      for h in range(HR):
                          s_, r_, g_, y_ = sqo[h % 2], rso[h % 2], grl[h % 2], yo[h % 2]
                          oh = OTs.ap[:, h * GS:(h + 1) * GS]
                          dma("sp", g_.ap, GRT[h * 128:(h + 1) * 128, t0:t0 + GS], reads=[scr], writes=[g_])
                          act(s_.ap, oh, AF.Square, reads=[OTs], writes=[s_])
                          mm(PS[6].ap[:, 0:GS], ones, s_.ap, True, True, reads=[s_, consts], writes=[PS[6]])
                          ts("dve", r_.ap, PS[6].ap[:, 0:GS], 1.0 / 128, EPS, ALU.mult, ALU.add, reads=[PS[6]], writes=[r_])
                          rsq(r_)
                          tt("dve", r_.ap, r_.ap, oh, ALU.mult, reads=[r_, OTs], writes=[r_])
                          stt(y_.ap, r_.ap, rnwT.ap[:, l:l + 1], g_.ap, ALU.mult, ALU.mult, reads=[r_, g_, rnwT],
                              writes=[y_])
                          dma("pool", MIXT[h * 128:(h + 1) * 128, t0:t0 + GS], y_.ap, reads=[y_], appends=[scr])
              if not emit:
                  dma("pool", SEND.rearrange("(h p) v -> p h v", p=128), SS.ap.rearrange("p (h v) -> p h v", h=HR),
                      reads=[SS], writes=[sendb])
                  S.op("pool", lambda e: e.collective_compute("AllGather", ALU.bypass,
                                                              replica_groups=[[0, 1], [2, 3], [4, 5], [6, 7]],
                                                              ins=[SEND], outs=[SGAT]), reads=[sendb], writes=[sendb],
                       kind="cc")
                  dma("pool", SS.ap.rearrange("p (h v) -> p h v", h=HR),
                      SGAT[0:WR, :].rearrange("(h p) v -> p h v", p=128), reads=[sendb, SS], writes=[SS])
                  ts("dve", SS.ap, SS.ap, sel.ap[:, 5:6], None, ALU.mult, reads=[SS, sel], writes=[SS])

          chk("D%d" % l)
          phase()
          Wout = wfull[("out", l)]
          mT = A.alloc(KM * TT, BF16)
          wt = [A.alloc(KM * 256, BF16) for _ in range(3)]
          xres = [A.alloc(TT, F32) for _ in range(3)]
          xo = [A.alloc(TT, F32) for _ in range(3)]
          wi = 0
          oi = 0
          g1 = mod(l, 2)
          for ttile in range(NTT):
              t0 = ttile * TT
              dma("sp", mT.ap.rearrange("p (k t) -> p k t", k=KM),
                  MIXT[:, t0:t0 + TT].rearrange("(k p) t -> p k t", p=128), reads=[scr], writes=[mT])
              for cb in range(0, D, 256):
                  w = wt[wi % 3]
                  wi += 1
                  dma("sp", w.ap.rearrange("p (k n) -> p k n", k=KM),
                      Wout[:, cb:cb + 256].rearrange("(k p) n -> p k n", p=128), reads=[wb[("out", l)]], writes=[w])
                  for hb in range(2):
                      nb0 = (cb + hb * 128) // 128
                      ps = PS[oi % 4]
                      xr_, xo_ = xres[oi % 3], xo[oi % 3]
                      oi += 1
                      dma("sp", xr_.ap, Xa[nb0 * 128:(nb0 + 1) * 128, t0:t0 + TT], reads=[xt_b], writes=[xr_])
                      for kc in range(KM):
                          mm(ps.ap, w.ap[:, kc * 256 + hb * 128:kc * 256 + (hb + 1) * 128],
                             mT.ap[:, kc * TT:(kc + 1) * TT], kc == 0, kc == KM - 1, reads=[w, mT], writes=[ps])
                      stt(xo_.ap, ps.ap, g1[:, nb0:nb0 + 1], xr_.ap, ALU.mult, ALU.add, reads=[ps, xr_, modT],
                          writes=[xo_])
                      dma("pool", Xb[nb0 * 128:(nb0 + 1) * 128, t0:t0 + TT], xo_.ap, reads=[xo_], appends=[xt_b])

          chk("E%d" % l)
          phase()
          Wup, Wdn = wfull[("up", l)], wfull[("dn", l)]
          hT = A.alloc(KC * TF, BF16)
          uT = A.alloc(FC * TF, BF16)
          xr = [A.alloc(min(4, KC) * TF, F32) for _ in range(2)]
          GF = min(8, FC)
          wt = [A.alloc(max(KC * 256, GF * 512), BF16) for _ in range(3)]
          rtmp = [A.alloc(TF, F32) for _ in range(2)]
          xres = [A.alloc(TF, F32) for _ in range(2)]
          xo = [A.alloc(TF, F32) for _ in range(2)]
          yst = [A.alloc(512, F32) for _ in range(2)]
          g2 = mod(l, 5)
          ptm = prenorm_tmps(TF)
          wi = 0
          oi = 0
          for ttile in range(NT // TF):
              t0 = ttile * TF
              prenorm(l, 1, Xb, t0, hT, xr, TF, ptm)
              for cb in range(0, DFF, 256):
                  w = wt[wi % 3]
                  wi += 1
                  dma("sp", w.ap[:, 0:KC * 256].rearrange("p (k n) -> p k n", k=KC),
                      Wup[:, cb:cb + 256].rearrange("(k p) n -> p k n", p=128), reads=[wb[("up", l)]], writes=[w])
                  for hb in range(2):
                      fc = (cb + hb * 128) // 128
                      ps = PS[oi % 2]
                      r_ = rtmp[oi % 2]
                      oi += 1
                      for kc in range(KC):
                          mm(ps.ap[:, 0:TF], w.ap[:, kc * 256 + hb * 128:kc * 256 + (hb + 1) * 128],
                             hT.ap[:, kc * TF:(kc + 1) * TF], kc == 0, kc == KC - 1, reads=[w, hT], writes=[ps])
                      act(r_.ap, ps.ap[:, 0:TF], AF.Relu, reads=[ps], writes=[r_])
                      tt("dve", uT.ap[:, fc * TF:(fc + 1) * TF], r_.ap, r_.ap, ALU.mult, reads=[r_], appends=[uT])
              for n0 in range(0, D, 512):
                  for fg_ in range(0, FC, GF):
                      w = wt[wi % 3]
                      wi += 1
                      dma("sp", w.ap[:, 0:GF * 512].rearrange("p (k n) -> p k n", k=GF),
                          Wdn[fg_ * 128:(fg_ + GF) * 128, n0:n0 + 512].rearrange("(k p) n -> p k n", p=128),
                          reads=[wb[("dn", l)]], writes=[w])
                      for f in range(GF):
                          fc = fg_ + f
                          for nb in range(4):
                              mm(PS[2 + nb].ap[:, 0:TF], w.ap[:, f * 512 + nb * 128:f * 512 + (nb + 1) * 128],
                                 uT.ap[:, fc * TF:(fc + 1) * TF], fc == 0, fc == FC - 1, reads=[w, uT],
                                 writes=[PS[2 + nb]])
                  for nb in range(4):
                      nb0 = n0 // 128 + nb
                      xr_, xo_ = xres[nb % 2], xo[nb % 2]
                      dma("sp", xr_.ap, Xb[nb0 * 128:(nb0 + 1) * 128, t0:t0 + TF], reads=[xt_b], writes=[xr_])
                      stt(xo_.ap, PS[2 + nb].ap[:, 0:TF], g2[:, nb0:nb0 + 1], xr_.ap, ALU.mult, ALU.add,
                          reads=[PS[2 + nb], xr_, modT], writes=[xo_])
                      if l == 0:
                          dma("pool", Xa[nb0 * 128:(nb0 + 1) * 128, t0:t0 + TF], xo_.ap, reads=[xo_], appends=[xt_b])
                      else:
                          ys = yst[nb % 2]
                          for tsb in range(TF // 128):
                              tr(PS[6].ap[:, tsb * 128:(tsb + 1) * 128], xo_.ap[:, tsb * 128:(tsb + 1) * 128], ident,
                                 reads=[xo_, consts], writes=[PS[6]])
                          cp("act", ys.ap[:, 0:TF], PS[6].ap[:, 0:TF], reads=[PS[6]], writes=[ys])
                          dma("pool", y_out[t0:t0 + TF, nb0 * 128:(nb0 + 1) * 128].rearrange("(s p) n -> p s n", p=128),
                              ys.ap[:, 0:TF].rearrange("p (s n) -> p s n", s=TF // 128), reads=[ys], appends=[xt_b])
    S.barrier()
    S.emit(nc, stack)
    stack.close()
    return nc


def make_consts():
    c = np.zeros((128, 384), np.float32)
    c[:, 0:128] = np.eye(128, dtype=np.float32)
    c[:, 128:256] = np.triu(np.ones((128, 128), np.float32))
    c[:, 256:384] = 1.0
    sb = np.zeros((16, 16, 128), np.float32)
    for h in range(16):
        sb[h, h, :] = math.sqrt(128.0)
    return c, sb.reshape(16, 2048)


def make_in_maps(cfg, x, c, lower_bounds, w_ada, b_ada, norm_mix_w, norm_ffn_w, w_in, rec_norm_w, fg_bias,
                 q_norm_w, k_norm_w, w_out, w_up, w_down):
    f = lambda a: np.ascontiguousarray(np.asarray(a, dtype=np.float32))
    D, NT, KC, DR = cfg.D, cfg.NT, cfg.KC, cfg.DR
    consts, selb = make_consts()
    x, c, w_ada, w_in, w_out, w_up, w_down = f(x), f(c), f(w_ada), f(w_in), f(w_out), f(w_up), f(w_down)
    cT = np.ascontiguousarray(c.T)
    maps = []
    AGLIM = cfg.AGLIM

    def p2floor(v):
        p = 1
        while p * 2 <= v:
            p *= 2
        return p

    def bc(w, R, r):
        L, rows, C = w.shape
        rc = min(R, p2floor(AGLIM // (cfg.WS * C * 2)))
        return np.ascontiguousarray(w.reshape(L, rows // (cfg.WS * rc), cfg.WS, rc, C)[:, :, r].reshape(L, R, C))

    for core in range(NCORES):
        b, s = core // 2, core % 2
        sel = np.zeros((128, 8), np.float32)
        sel[:, b] = 1.0
        sel[:, 4] = 0.0 if s == 1 else -1e30
        sel[:, 5] = 1.0 if s == 1 else 0.0
        r = core % cfg.WS
        mo = cfg.MIXW // cfg.WS
        fo = cfg.DFF // cfg.WS
        maps.append({
            "x": f(x[b, s * NT:(s + 1) * NT, :]),
            "cT": f(cT[r * DR:(r + 1) * DR, :]),
            "w_ada": f(w_ada[:, r * DR:(r + 1) * DR, :]),
            "b_ada": f(b_ada).reshape(12 * KC, 128),
            "norm_mix_w": f(norm_mix_w).reshape(2 * KC, 128),
            "norm_ffn_w": f(norm_ffn_w).reshape(2 * KC, 128),
            "lower_bounds": f(lower_bounds).reshape(2 * cfg.HR, 128),
            "rec_norm_w": f(rec_norm_w), "q_norm_w": f(q_norm_w), "k_norm_w": f(k_norm_w),
            "fg_bias": f(fg_bias).reshape(1, 2 * cfg.HA),
            "w_in": bc(w_in, DR, r), "w_out": bc(w_out, mo, r), "w_up": bc(w_up, DR, r), "w_down": bc(w_down, fo, r),
            "consts": consts, "selb": selb, "sel": sel,
        })
    return maps


def run(cfg, **inputs):
    nc = build(cfg)
    maps = make_in_maps(cfg, **inputs)
    res = run_bass_kernel_spmd(nc, maps, core_ids=list(range(NCORES)))
    out = np.zeros((cfg.B, cfg.SEQ, cfg.D), np.float32)
    for core in range(NCORES):
        b, s = core // 2, core % 2
        out[b, s * cfg.NT:(s + 1) * cfg.NT, :] = res.results[core]["y"]
    return out


def kernel(**inputs):
    return run(Cfg(), **inputs)
```

```python
import math
import numpy as np
import concourse.bass as bass
import concourse.mybir as mybir
from concourse.bass_utils import run_bass_kernel_spmd

F32, BF16 = mybir.dt.float32, mybir.dt.bfloat16
AF = mybir.ActivationFunctionType
ALU = mybir.AluOpType
EPS = 1e-6
K_MAX = 1.0 - 1e-6
NCORES = 8
ARENA_BYTES = 204 * 1024


class Cfg:
    def __init__(self, D=4096, SEQ=4096, B=4, DFF=None):
        self.D, self.SEQ, self.B = D, SEQ, B
        self.NT = SEQ // 2
        self.KC = D // 128
        self.NH = D // 128
        self.HR = self.NH // 2
        self.HA = self.NH - self.HR
        self.WR, self.WA = self.HR * 128, self.HA * 128
        self.MIXW = self.WR + self.WA
        self.KM = self.MIXW // 128
        self.DFF = DFF or 4 * D
        self.FC = self.DFF // 128
        self.IC = 4 * self.WR + 3 * self.WA + self.HA
        self.WS = 4
        self.AGLIM = 4 * 1024 * 1024
        self.DR = D // 4
        self.TT = 512
        self.TF = 512
        self.NTT = self.NT // 512
        self.NB = self.NT // 128
        self.CH = 32
        self.NCH = self.NT // self.CH
        self.stop = None
        self.dump = None


class Op:
    __slots__ = ("eng", "fn", "deps", "signal", "sem", "cnt", "kind", "idx")


class Buf:
    REG = []

    def __init__(self):
        self.writers, self.readers = [], []
        Buf.REG.append(self)


class Tile(Buf):
    def __init__(self, ap):
        Buf.__init__(self)
        self.ap = ap


class Sched:
    def __init__(self):
        self.ops = []
        self.ring = {"sp": 24, "pool": 16}
        self.ring_i = {"sp": 0, "pool": 0}
        self.ring_last = {}
        self.last = {}
        self.stopped = False

    def op(self, eng, fn, reads=(), writes=(), appends=(), kind="c"):
        if self.stopped:
            return None
        o = Op()
        o.eng, o.fn, o.kind, o.signal, o.sem, o.cnt = eng, fn, kind, False, None, 0
        deps = []
        for b in reads:
            deps += b.writers
        for b in writes:
            deps += b.writers + b.readers
        for b in appends:
            deps += b.readers
        if kind == "d":
            slot = (eng, self.ring_i[eng] % self.ring[eng])
            self.ring_i[eng] += 1
            if slot in self.ring_last:
                deps.append(self.ring_last[slot])
            self.ring_last[slot] = o
            o.sem = slot
            o.signal = True
        elif kind == "cc":
            o.sem = ("cc", 0)
            o.signal = True
        else:
            o.sem = (eng, -1)
        best = {}
        for d in deps:
            if d.eng == "pe" and eng == "pe" and d.kind == "c" and kind == "c":
                continue
            p = best.get(d.sem)
            if p is None or p.idx < d.idx:
                best[d.sem] = d
        dl = list(best.values())
        for d in dl:
            d.signal = True
        o.deps = dl
        o.idx = len(self.ops)

        def add(lst, o):
            lst[:] = [x for x in lst if x.sem != o.sem]
            lst.append(o)

        for b in reads:
            add(b.readers, o)
        for b in writes:
            b.writers = [o]
            b.readers = []
        for b in appends:
            add(b.writers, o)
        self.ops.append(o)
        if kind == "c" or kind == "cc":
            self.last[o.sem] = o
        return o

    def barrier(self):
        if self.stopped:
            return
        tails = list(self.last.values()) + list(self.ring_last.values())
        for e in ("sp", "pe", "act", "dve", "pool"):
            o = Op()
            o.eng, o.fn, o.kind, o.signal, o.sem, o.cnt = e, None, "b", False, None, 0
            o.idx = len(self.ops)
            o.deps = list(tails)
            for d in tails:
                d.signal = True
            self.ops.append(o)
        for b in Buf.REG:
            b.writers, b.readers = [], []

    def emit(self, nc, stack):
        sems = {}

        def getsem(key):
            if key not in sems:
                sems[key] = stack.enter_context(nc.semaphore("s_%s_%d" % (key[0], key[1] + 1)))
            return sems[key]

        counts = {}
        for o in self.ops:
            if o.signal:
                inc = 16 if o.kind == "d" else 1
                counts[o.sem] = counts.get(o.sem, 0) + inc
                o.cnt = counts[o.sem]
                getsem(o.sem)
        block = stack.enter_context(nc.Block())
        engmap = {"sp": block.sync, "pe": block.tensor, "act": block.scalar, "dve": block.vector,
                  "pool": block.gpsimd}
        ops = self.ops
        for ename, deco in engmap.items():
            def run(e, ename=ename):
                waited = {}
                for o in ops:
                    if o.eng != ename:
                        continue
                    need = {}
                    for d in o.deps:
                        if need.get(d.sem, 0) < d.cnt:
                            need[d.sem] = d.cnt
                    for sk, c in need.items():
                        if waited.get(sk, 0) < c:
                            e.wait_ge(sems[sk], c)
                            waited[sk] = c
                    if o.fn is not None:
                        ins = o.fn(e)
                        if o.signal:
                            if o.kind == "d":
                                ins.then_inc(sems[o.sem], 16)
                            elif o.kind == "cc":
                                ins.then_inc(sems[o.sem])
                            else:
                                ins.then_inc(sems[o.sem], 1)
            deco(run)


class Arena:
    def __init__(self, big):
        self.big = big
        self.base = 0
        self.ptr = 0

    def alloc(self, cols, dt, parts=128):
        nb = cols * (4 if dt == F32 else 2)
        nb = (nb + 63) // 64 * 64
        assert self.ptr + nb <= ARENA_BYTES, ("arena overflow", self.ptr, nb)
        a = self.big[:, self.ptr // 4:(self.ptr + nb) // 4]
        self.ptr += nb
        if dt != F32:
            a = a.bitcast(dt)
        a = a[0:parts, 0:cols]
        return Tile(a)

    def persist(self):
        self.base = self.ptr

    def reset(self):
        self.ptr = self.base


def build(cfg):
    Buf.REG = []
    from contextlib import ExitStack
    c = cfg
    D, NT, KC, HR, HA, WR, WA, DFF, FC, IC, DR = c.D, c.NT, c.KC, c.HR, c.HA, c.WR, c.WA, c.DFF, c.FC, c.IC, c.DR
    MIXW, KM, TT, NTT, NB, NCH = c.MIXW, c.KM, c.TT, c.NTT, c.NB, c.NCH
    WS = c.WS
    AGLIM = c.AGLIM

    def p2floor(v):
        p = 1
        while p * 2 <= v:
            p *= 2
        return p

    def wchunk(r, cc_):
        return min(r, p2floor(AGLIM // (WS * cc_ * 2)))

    RCK = min(c.WA, p2floor(AGLIM // (2 * c.NT * 2)))
    RCV = min(c.NT, max(128, p2floor(AGLIM // (2 * c.WA * 2))))
    GS = 128
    NCG = GS // c.CH
    TF = c.TF
    CH = c.CH
    NC5 = 512 // CH
    G4 = [[0, 1, 2, 3], [4, 5, 6, 7]]
    nc = bass.Bass("TRN2", target_bir_lowering=False)
    stack = ExitStack()

    def din(name, shape, dt=F32):
        return nc.dram_tensor(name, list(shape), dt, kind="ExternalInput").ap()

    def dscr(name, shape, dt):
        return nc.dram_tensor(name, list(shape), dt).ap()

    x_in = din("x", [NT, D])
    cT_in = din("cT", [DR, 4])
    wada_in = din("w_ada", [2, DR, 6 * D])
    bada_in = din("b_ada", [12 * KC, 128])
    nwm_in = din("norm_mix_w", [2 * KC, 128])
    nwf_in = din("norm_ffn_w", [2 * KC, 128])
    lb_in = din("lower_bounds", [2 * HR, 128])
    rnw_in = din("rec_norm_w", [2, 128])
    qnw_in = din("q_norm_w", [2, 128])
    knw_in = din("k_norm_w", [2, 128])
    fgb_in = din("fg_bias", [1, 2 * HA])
    win_in = din("w_in", [2, DR, IC])
    wout_in = din("w_out", [2, MIXW // WS, D])
    wup_in = din("w_up", [2, DR, DFF])
    wdn_in = din("w_down", [2, DFF // WS, D])
    consts_in = din("consts", [128, 384])
    selb_in = din("selb", [16, 16 * 128])
    sel_in = din("sel", [128, 8])
    y_out = nc.dram_tensor("y", [NT, D], F32, kind="ExternalOutput").ap()

    wsh = {}
    wfull = {}
    wspec = {"in": (win_in, DR, IC), "out": (wout_in, MIXW // WS, D), "up": (wup_in, DR, DFF),
             "dn": (wdn_in, DFF // WS, D)}
    for l in range(2):
        for k, (_, r, cc_) in wspec.items():
            wsh[(k, l)] = dscr("wsh_%s%d" % (k, l), [r, cc_], BF16)
            wfull[(k, l)] = dscr("wfull_%s%d" % (k, l), [WS * r, cc_], BF16)
    NM = 2 * 6 * KC * 4
    pm_d = dscr("pm_d", [128, NM], F32)
    pmg_d = dscr("pmg_d", [WS * 128, NM], F32)
    XT = [dscr("XT%d" % i, [D, NT], F32) for i in range(2)]
    QRT = dscr("QRT", [WR, NT], BF16)
    ZRT = dscr("ZRT", [WR, NT], F32)
    GRT = dscr("GRT", [WR, NT], BF16)
    IRd = dscr("IRd", [NT, WR], BF16)
    QAT = dscr("QAT", [WA, NT], F32)
    KAT = dscr("KAT", [WA, NT], F32)
    VAd = dscr("VAd", [NT, WA], BF16)
    VA2 = dscr("VA2", [2 * NT, WA], BF16)
    FGd = dscr("FGd", [NT, HA], F32)
    LFd = dscr("LFd", [NT, HA], F32)
    LF2 = dscr("LF2", [2 * NT, HA], F32)
    QNT = dscr("QNT", [WA, NT], BF16)
    KNT = dscr("KNT", [WA, NT], BF16)
    KN2 = dscr("KN2", [2 * WA, NT], BF16)
    QGT = dscr("QGT", [WR, NT], BF16)
    KGT = dscr("KGT", [WR, NT], BF16)
    SEND = dscr("SEND", [WR, 128], F32)
    SGAT = dscr("SGAT", [2 * WR, 128], F32)
    MIXT = dscr("MIXT", [MIXW, NT], BF16)

    big = stack.enter_context(nc.sbuf_tensor("big", [128, ARENA_BYTES // 4], F32))
    PS = [Tile(stack.enter_context(nc.psum_tensor("ps%d" % i, [128, 512], F32))[:]) for i in range(7)]
    PSB = Tile(stack.enter_context(nc.psum_tensor("psb", [128, 1024], BF16))[:])
    A = Arena(big)
    S = Sched()

    def dma(eng, out, in_, reads=(), writes=(), appends=(), **kw):
        return S.op(eng, lambda e: e.dma_start(out=out, in_=in_, **kw), reads, writes, appends, kind="d")

    def mm(out, lhsT, rhs, start, stop, reads=(), writes=(), appends=()):
        return S.op("pe", lambda e: e.matmul(out, lhsT, rhs, start=start, stop=stop), reads, writes, appends)

    def tr(out, in_, ident, reads=(), writes=(), appends=()):
        return S.op("pe", lambda e: e.transpose(out, in_, ident), reads, writes, appends)

    def act(out, in_, func, reads=(), writes=(), bias=None, scale=None, appends=()):
        kw = {}
        if bias is not None:
            kw["bias"] = bias
        if scale is not None:
            kw["scale"] = scale
        return S.op("act", lambda e: e.activation(out=out, in_=in_, func=func, **kw), reads, writes, appends)

    def ts(eng, out, in0, s1, s2, op0, op1=None, reads=(), writes=(), appends=()):
        if op1 is None:
            return S.op(eng, lambda e: e.tensor_scalar(out=out, in0=in0, scalar1=s1, scalar2=None, op0=op0),
                        reads, writes, appends)
        return S.op(eng, lambda e: e.tensor_scalar(out=out, in0=in0, scalar1=s1, scalar2=s2, op0=op0, op1=op1),
                    reads, writes, appends)

    def tt(eng, out, in0, in1, op, reads=(), writes=(), appends=()):
        return S.op(eng, lambda e: e.tensor_tensor(out=out, in0=in0, in1=in1, op=op), reads, writes, appends)

    def stt(out, in0, scalar, in1, op0, op1, reads=(), writes=(), appends=()):
        return S.op("dve", lambda e: e.scalar_tensor_tensor(out=out, in0=in0, scalar=scalar, in1=in1,
                                                            op0=op0, op1=op1), reads, writes, appends)

    def cp(eng, out, in_, reads=(), writes=(), appends=()):
        if eng == "act":
            return act(out, in_, AF.Identity, reads, writes, appends=appends)
        return S.op(eng, lambda e: e.tensor_copy(out=out, in_=in_), reads, writes, appends)

    def rsq(T):
        act(T.ap, T.ap, AF.Sqrt, reads=[T], writes=[T])
        S.op("dve", lambda e: e.reciprocal(out=T.ap, in_=T.ap), reads=[T], writes=[T])

    def phase():
        S.barrier()
        A.reset()

    def chk(name):
        if c.stop == name and not S.stopped:
            S.barrier()
            src = scr_names[c.dump]
            r, cc_ = src.shape
            r = min(r, NT)
            cc_ = min(cc_, D)
            dma("pool", y_out[0:r, 0:cc_], src[0:r, 0:cc_], max_dma_last_dim=2048)
            S.barrier()
            S.stopped = True

    scr_names = dict(XT0=XT[0], XT1=XT[1], QRT=QRT, ZRT=ZRT, GRT=GRT, IRd=IRd, QAT=QAT, KAT=KAT, VAd=VAd, VA2=VA2, FGd=FGd,
                     LFd=LFd, LF2=LF2, QNT=QNT, KNT=KNT, KN2=KN2, QGT=QGT, KGT=KGT, SEND=SEND, SGAT=SGAT, MIXT=MIXT,
                     pm_d=pm_d, pmg_d=pmg_d, wfull_in0=wfull[("in", 0)])

    consts = A.alloc(384, F32)
    ident, tri, ones = consts.ap[:, 0:128], consts.ap[:, 128:256], consts.ap[:, 256:384]
    cb16 = A.alloc(384, BF16)
    identb, trib, onesb = cb16.ap[:, 0:128], cb16.ap[:, 128:256], cb16.ap[:, 256:384]
    sel = A.alloc(8, F32)
    modT = A.alloc(12 * KC, F32)
    gsc = A.alloc(4 * KC, F32)
    nwT = A.alloc(4 * KC, F32)
    lbT = A.alloc(2 * HR, F32)
    omlb = A.alloc(2 * HR, F32)
    rnwT = A.alloc(2, F32)
    qnwT = A.alloc(2, F32)
    knwT = A.alloc(2, F32)
    fgb = A.alloc(2 * HA, F32)
    Mb = A.alloc(4, F32)
    onesrow = A.alloc(64, F32)
    negm = A.alloc(128, F32)
    A.persist()

    def mod(l, j):
        return modT.ap[:, (l * 6 + j) * KC:(l * 6 + j + 1) * KC]

    dma("sp", consts.ap, consts_in, writes=[consts])
    dma("sp", sel.ap, sel_in, writes=[sel])
    cp("dve", cb16.ap, consts.ap, reads=[consts], writes=[cb16])
    dma("sp", fgb.ap, fgb_in[0, :].partition_broadcast(128), writes=[fgb])
    cp("dve", onesrow.ap, consts.ap[:, 256:320], reads=[consts], writes=[onesrow])
    ts("dve", negm.ap, consts.ap[:, 128:256], -1.0, 1e30, ALU.add, ALU.mult, reads=[consts], writes=[negm])

    def load_T(dst_tile, dst_ap, src, R):
        r0 = 0
        while r0 < R:
            r = min(128, R - r0)
            t = A.alloc(128, F32)
            dma("sp", t.ap[0:r, :], src[r0:r0 + r, :], writes=[t])
            tr(PS[0].ap[:, 0:r], t.ap[0:r, :], ident[0:r, 0:r], reads=[t, consts], writes=[PS[0]])
            cp("dve", dst_ap[:, r0:r0 + r], PS[0].ap[:, 0:r], reads=[PS[0]], appends=[dst_tile])
            r0 += r

    badaT = A.alloc(12 * KC, F32)
    load_T(badaT, badaT.ap, bada_in, 12 * KC)
    load_T(nwT, nwT.ap[:, 0:2 * KC], nwm_in, 2 * KC)
    load_T(nwT, nwT.ap[:, 2 * KC:4 * KC], nwf_in, 2 * KC)
    load_T(lbT, lbT.ap, lb_in, 2 * HR)
    load_T(rnwT, rnwT.ap, rnw_in, 2)
    load_T(qnwT, qnwT.ap, qnw_in, 2)
    load_T(knwT, knwT.ap, knw_in, 2)
    S.op("dve", lambda e: e.memset(omlb.ap[:, 0:HR], 1.0), writes=[omlb])
    dl = A.alloc(HR, F32)
    tt("dve", dl.ap, lbT.ap[:, 0:HR], lbT.ap[:, HR:2 * HR], ALU.subtract, reads=[lbT], writes=[dl])
    act(omlb.ap[:, HR:2 * HR], dl.ap, AF.Sigmoid, reads=[dl], appends=[omlb])
    for l in range(2):
        rq = A.alloc(128, F32, parts=1)
        rk = A.alloc(128, F32, parts=1)
        mq = A.alloc(4, F32, parts=1)
        dma("sp", rq.ap, qnw_in[l:l + 1, :], writes=[rq])
        dma("sp", rk.ap, knw_in[l:l + 1, :], writes=[rk])
        tt("dve", rq.ap, rq.ap, rq.ap, ALU.mult, reads=[rq], writes=[rq])
        tt("dve", rk.ap, rk.ap, rk.ap, ALU.mult, reads=[rk], writes=[rk])
        S.op("dve", lambda e, rq=rq, mq=mq: e.tensor_reduce(out=mq.ap[:, 0:1], in_=rq.ap, axis=mybir.AxisListType.X,
                                                         op=ALU.max), reads=[rq], writes=[mq])
        S.op("dve", lambda e, rk=rk, mq=mq: e.tensor_reduce(out=mq.ap[:, 1:2], in_=rk.ap, axis=mybir.AxisListType.X,
                                                         op=ALU.max), reads=[rk, mq], writes=[mq])
        tt("dve", mq.ap[:, 2:3], mq.ap[:, 0:1], mq.ap[:, 1:2], ALU.mult, reads=[mq], writes=[mq])
        tt("dve", mq.ap[:, 3:4], mq.ap[:, 0:1], mq.ap[:, 1:2], ALU.mult, reads=[mq], writes=[mq])
        act(mq.ap[:, 2:4], mq.ap[:, 2:4], AF.Sqrt, reads=[mq], writes=[mq])
        mm(PS[1].ap[:, 0:2], consts.ap[0:1, 256:384], mq.ap[:, 2:4], True, True,
           reads=[mq, consts], writes=[PS[1]])
        ts("dve", Mb.ap[:, 2 * l:2 * l + 1], PS[1].ap[:, 0:1], math.sqrt(128.0), None, ALU.mult, reads=[PS[1]],
           appends=[Mb])
        ts("dve", Mb.ap[:, 2 * l + 1:2 * l + 2], PS[1].ap[:, 0:1], -math.sqrt(128.0), None, ALU.mult, reads=[PS[1]],
           appends=[Mb])

    chk("p0")
    kchunks = (DR + 127) // 128
    kr = min(128, DR)
    scT = A.alloc(4 * kchunks, F32)
    for kc in range(kchunks):
        dma("sp", scT.ap[0:kr, kc * 4:(kc + 1) * 4], cT_in[kc * kr:(kc + 1) * kr, :], appends=[scT])
    scS = A.alloc(4 * kchunks, F32)
    act(scS.ap[0:kr, :], scT.ap[0:kr, :], AF.Silu, reads=[scT], writes=[scS])
    pmod = A.alloc(NM, F32)
    wa = [A.alloc(kchunks * 512, F32) for _ in range(2)]
    ncb = 6 * D // 512
    it = 0
    for l in range(2):
        for cb in range(ncb):
            w = wa[it % 2]
            dma("sp", w.ap[0:kr, :].rearrange("p (k n) -> p k n", k=kchunks),
                wada_in[l, :, cb * 512:(cb + 1) * 512].rearrange("(k p) n -> p k n", p=kr), writes=[w])
            ps = PS[2 + it % 2]
            for sb in range(4):
                for kc in range(kchunks):
                    mm(ps.ap[:, sb * 4:(sb + 1) * 4], w.ap[0:kr, kc * 512 + sb * 128:kc * 512 + (sb + 1) * 128],
                       scS.ap[0:kr, kc * 4:(kc + 1) * 4], kc == 0, kc == kchunks - 1, reads=[w, scS], writes=[ps])
            off = (l * ncb + cb) * 16
            cp("dve", pmod.ap[:, off:off + 16], ps.ap[:, 0:16], reads=[ps], appends=[pmod])
            it += 1
    pmb = Buf()
    dma("pool", pm_d, pmod.ap, reads=[pmod], writes=[pmb])
    chk("p1a")
    S.op("pool", lambda e: e.collective_compute("AllGather", ALU.bypass, replica_groups=G4,
                                                ins=[pm_d], outs=[pmg_d]), reads=[pmb], writes=[pmb], kind="cc")
    pg = A.alloc(WS * NM, F32)
    dma("pool", pg.ap.rearrange("p (r n) -> p r n", r=WS), pmg_d.rearrange("(r p) n -> p r n", p=128),
        reads=[pmb], writes=[pg])
    for r in range(1, WS):
        tt("dve", pg.ap[:, 0:NM], pg.ap[:, 0:NM], pg.ap[:, r * NM:(r + 1) * NM], ALU.add, reads=[pg], writes=[pg])
    pgv = pg.ap[:, 0:NM].rearrange("p (m b) -> p m b", b=4)
    cp("dve", modT.ap, badaT.ap, reads=[badaT], writes=[modT])
    for b in range(4):
        stt(modT.ap, pgv[:, :, b], sel.ap[:, b:b + 1], modT.ap, ALU.mult, ALU.add, reads=[pg, sel, modT],
            writes=[modT])
    for l in range(2):
        for sub in range(2):
            sc_ = mod(l, 3 * sub + 1)
            g = gsc.ap[:, (l * 2 + sub) * KC:(l * 2 + sub + 1) * KC]
            nw = nwT.ap[:, (sub * 2 + l) * KC:(sub * 2 + l + 1) * KC]
            stt(g, sc_, 1.0, nw, ALU.add, ALU.mult, reads=[modT, nwT], appends=[gsc])

    chk("p1")
    wb = {}
    for l in range(2):
        for k in ("in", "out", "up", "dn"):
            src, r, cc_ = wspec[k]
            b = Buf()
            wb[(k, l)] = b
            r0 = 0
            while r0 < r:
                rr = min(128, r - r0)
                dma("pool", wsh[(k, l)][r0:r0 + rr, :], src[l, r0:r0 + rr, :], appends=[b], max_dma_last_dim=8192)
                r0 += rr
            rc = wchunk(r, cc_)
            for j in range(r // rc):
                S.op("pool", lambda e, k=k, l=l, j=j, rc=rc: e.collective_compute(
                    "AllGather", ALU.bypass, replica_groups=G4, ins=[wsh[(k, l)][j * rc:(j + 1) * rc, :]],
                    outs=[wfull[(k, l)][WS * j * rc:(WS * j + WS) * rc, :]]),
                    reads=[b], appends=[b], kind="cc")

    chk("p2")
    phase()
    xt_b = Buf()
    xin = [A.alloc(D, F32) for _ in range(2)]
    xst = [A.alloc(KC * 128, F32) for _ in range(2)]
    for tb in range(NB):
        xi, xs = xin[tb % 2], xst[tb % 2]
        dma("sp", xi.ap, x_in[tb * 128:(tb + 1) * 128, :], writes=[xi])
        for k4 in range(0, KC, 4):
            ps = PS[(k4 // 4) % 4]
            for j in range(4):
                tr(ps.ap[:, j * 128:(j + 1) * 128], xi.ap[:, (k4 + j) * 128:(k4 + j + 1) * 128], ident,
                   reads=[xi, consts], writes=[ps])
            cp("act" if (k4 // 4) % 2 else "dve", xs.ap[:, k4 * 128:(k4 + 4) * 128], ps.ap, reads=[ps], appends=[xs])
        dma("pool", XT[0][:, tb * 128:(tb + 1) * 128].rearrange("(k p) t -> p k t", p=128),
            xs.ap.rearrange("p (k t) -> p k t", k=KC), reads=[xs], appends=[xt_b])

    def prenorm_tmps(tsz, n=2):
        return dict(sq=[A.alloc(tsz, F32) for _ in range(n)], rstd=A.alloc(tsz, F32),
                    tmp=[A.alloc(tsz, F32) for _ in range(n)])

    def prenorm(l, sub, XTin, t0, hT, xr, tsz, tm, G4=None):
        g = gsc.ap[:, (l * 2 + sub) * KC:(l * 2 + sub + 1) * KC]
        sh = mod(l, 3 * sub)
        G4 = G4 or min(4, KC)
        pss = PS[6]
        sq, rstd, tmp = tm["sq"], tm["rstd"], tm["tmp"]
        nq = len(sq)
        for k4 in range(0, KC, G4):
            xb = xr[(k4 // G4) % 2]
            dma("sp", xb.ap.rearrange("p (k t) -> p k t", k=G4),
                XTin[k4 * 128:(k4 + G4) * 128, t0:t0 + tsz].rearrange("(k p) t -> p k t", p=128), writes=[xb])
            for j in range(G4):
                s_ = sq[j % nq]
                act(s_.ap, xb.ap[:, j * tsz:(j + 1) * tsz], AF.Square, reads=[xb], writes=[s_])
                mm(pss.ap[:, 0:tsz], ones, s_.ap, k4 + j == 0, k4 + j == KC - 1, reads=[s_, consts], writes=[pss])
        ts("dve", rstd.ap, pss.ap[:, 0:tsz], 1.0 / D, EPS, ALU.mult, ALU.add, reads=[pss], writes=[rstd])
        rsq(rstd)
        for k4 in range(0, KC, G4):
            xb = xr[(k4 // G4) % 2]
            dma("sp", xb.ap.rearrange("p (k t) -> p k t", k=G4),
                XTin[k4 * 128:(k4 + G4) * 128, t0:t0 + tsz].rearrange("(k p) t -> p k t", p=128), writes=[xb])
            for j in range(G4):
                kc = k4 + j
                t_ = tmp[j % nq]
                tt("dve", t_.ap, xb.ap[:, j * tsz:(j + 1) * tsz], rstd.ap, ALU.mult, reads=[xb, rstd], writes=[t_])
                act(hT.ap[:, kc * tsz:(kc + 1) * tsz], t_.ap, AF.Identity, reads=[t_, gsc, modT], appends=[hT],
                    scale=g[:, kc:kc + 1], bias=sh[:, kc:kc + 1])

    chk("pro")
    if True:
      for l in range(2):
          Xa = XT[0]
          Xb = XT[1]
          phase()
          Win = wfull[("in", l)]
          hT = A.alloc(KC * TT, BF16)
          xr = [A.alloc(min(4, KC) * TT, F32) for _ in range(2)]
          wt = [A.alloc(KC * 256, BF16) for _ in range(3)]
          ost = [A.alloc(TT, F32) for _ in range(4)]
          wfg = A.alloc(KC * HA, BF16)
          ptm = prenorm_tmps(TT)
          scr = Buf()
          O_QR, O_FR, O_IR, O_GR = 0, WR, 2 * WR, 3 * WR
          O_QA = 4 * WR
          O_KA, O_VA, O_FG = O_QA + WA, O_QA + 2 * WA, O_QA + 3 * WA
          dma("sp", wfg.ap.rearrange("p (k n) -> p k n", k=KC),
              Win[:, O_FG:O_FG + HA].rearrange("(k p) n -> p k n", p=128), reads=[wb[("in", l)]], writes=[wfg])
          wi = 0
          oi = 0
          for ttile in range(NTT):
              t0 = ttile * TT
              prenorm(l, 0, Xa, t0, hT, xr, TT, ptm)
              fm_groups = [(O_QR, WR, "q"), (O_FR, WR, "z"), (O_GR, WR, "g"), (O_QA, WA, "qa"), (O_KA, WA, "ka")]
              for (c0, width, kind) in fm_groups:
                  for cb in range(0, width, 256):
                      w = wt[wi % 3]
                      wi += 1
                      dma("sp", w.ap.rearrange("p (k n) -> p k n", k=KC),
                          Win[:, c0 + cb:c0 + cb + 256].rearrange("(k p) n -> p k n", p=128),
                          reads=[wb[("in", l)]], writes=[w])
                      for hb in range(2):
                          ps = PS[oi % 4]
                          for kc in range(KC):
                              mm(ps.ap, w.ap[:, kc * 256 + hb * 128:kc * 256 + (hb + 1) * 128],
                                 hT.ap[:, kc * TT:(kc + 1) * TT], kc == 0, kc == KC - 1, reads=[w, hT], writes=[ps])
                          o = ost[oi % 4]
                          oi += 1
                          row = cb + hb * 128
                          if kind == "q":
                              ob = o.ap.bitcast(BF16)[:, 0:TT]
                              act(ob, ps.ap, AF.Silu, reads=[ps], writes=[o])
                              dma("pool", QRT[row:row + 128, t0:t0 + TT], ob, reads=[o], appends=[scr])
                          elif kind == "g":
                              ob = o.ap.bitcast(BF16)[:, 0:TT]
                              act(ob, ps.ap, AF.Silu, reads=[ps], writes=[o])
                              dma("pool", GRT[row:row + 128, t0:t0 + TT], ob, reads=[o], appends=[scr])
                          else:
                              dst = {"z": ZRT, "qa": QAT, "ka": KAT}[kind]
                              cp("dve", o.ap, ps.ap, reads=[ps], writes=[o])
                              dma("pool", dst[row:row + 128, t0:t0 + TT], o.ap, reads=[o], appends=[scr])
              for (c0, width, dst) in [(O_IR, WR, IRd), (O_VA, WA, VAd)]:
                  for cb in range(0, width, 256):
                      w = wt[wi % 3]
                      wi += 1
                      dma("sp", w.ap.rearrange("p (k n) -> p k n", k=KC),
                          Win[:, c0 + cb:c0 + cb + 256].rearrange("(k p) n -> p k n", p=128),
                          reads=[wb[("in", l)]], writes=[w])
                      for tsb in range(TT // 128):
                          ps = PS[oi % 4]
                          for kc in range(KC):
                              mm(ps.ap[:, 0:256], hT.ap[:, kc * TT + tsb * 128:kc * TT + (tsb + 1) * 128],
                                 w.ap[:, kc * 256:(kc + 1) * 256], kc == 0, kc == KC - 1, reads=[w, hT], writes=[ps])
                          o = ost[oi % 4]
                          oi += 1
                          ob = o.ap.bitcast(BF16)[:, 0:256]
                          cp("act", ob, ps.ap[:, 0:256], reads=[ps], writes=[o])
                          dma("pool", dst[t0 + tsb * 128:t0 + (tsb + 1) * 128, cb:cb + 256], ob, reads=[o],
                              appends=[scr])
              for tsb in range(TT // 128):
                  ps = PS[oi % 4]
                  for kc in range(KC):
                      mm(ps.ap[:, 0:HA], hT.ap[:, kc * TT + tsb * 128:kc * TT + (tsb + 1) * 128],
                         wfg.ap[:, kc * HA:(kc + 1) * HA], kc == 0, kc == KC - 1, reads=[wfg, hT], writes=[ps])
                  o = ost[oi % 4]
                  oi += 1
                  cp("dve", o.ap[:, 0:HA], ps.ap[:, 0:HA], reads=[ps], writes=[o])
                  dma("pool", FGd[t0 + tsb * 128:t0 + (tsb + 1) * 128, :], o.ap[:, 0:HA], reads=[o], appends=[scr])

          chk("A%d" % l)
          phase()
          scale = 1.0 / math.sqrt(128.0)
          selb = A.alloc(16 * 128, F32, parts=16)
          dma("sp", selb.ap, selb_in, writes=[selb])
          qin = [A.alloc(TT, F32) for _ in range(2)]
          sqb = [A.alloc(TT, F32) for _ in range(2)]
          rs = [A.alloc(TT, F32) for _ in range(2)]
          qo = [A.alloc(TT, BF16) for _ in range(2)]
          i = 0
          for (src, dst, wT) in [(QAT, QNT, qnwT), (KAT, KNT, knwT)]:
              for h in range(HA):
                  for ttile in range(NTT):
                      t0 = ttile * TT
                      q_, s_, r_, o_ = qin[i % 2], sqb[i % 2], rs[i % 2], qo[i % 2]
                      ps = PS[i % 2]
                      i += 1
                      dma("sp", q_.ap, src[h * 128:(h + 1) * 128, t0:t0 + TT], writes=[q_])
                      act(s_.ap, q_.ap, AF.Square, reads=[q_], writes=[s_])
                      mm(ps.ap, ones, s_.ap, True, True, reads=[s_, consts], writes=[ps])
                      ts("dve", r_.ap, ps.ap, 1.0 / 128, EPS, ALU.mult, ALU.add, reads=[ps], writes=[r_])
                      rsq(r_)
                      tt("dve", q_.ap, q_.ap, r_.ap, ALU.mult, reads=[q_, r_], writes=[q_])
                      act(o_.ap, q_.ap, AF.Identity, reads=[q_, wT], writes=[o_], scale=wT.ap[:, l:l + 1])
                      dma("pool", dst[h * 128:(h + 1) * 128, t0:t0 + TT], o_.ap, reads=[o_], appends=[scr])
          fg = A.alloc(NB * HA, F32)
          dma("sp", fg.ap.rearrange("p (b h) -> p b h", b=NB), FGd.rearrange("(b p) h -> p b h", p=128), writes=[fg])
          fgv = fg.ap.rearrange("p (b h) -> p b h", b=NB)
          for b in range(NB):
              tt("dve", fgv[:, b, :], fgv[:, b, :], fgb.ap[:, l * HA:(l + 1) * HA], ALU.add, reads=[fg, fgb],
                 writes=[fg])
          ab = A.alloc(NB * HA, F32)
          act(ab.ap, fg.ap, AF.Abs, reads=[fg], writes=[ab])
          act(ab.ap, ab.ap, AF.Exp, reads=[ab], writes=[ab], scale=-1.0)
          act(ab.ap, ab.ap, AF.Ln, reads=[ab], writes=[ab], bias=1.0)
          lfcat = A.alloc(2 * NB * HA, F32)
          lfo = lfcat.ap[:, NB * HA:2 * NB * HA]
          ts("dve", fg.ap, fg.ap, 0.0, None, ALU.min, reads=[fg], writes=[fg])
          tt("dve", lfo, fg.ap, ab.ap, ALU.subtract, reads=[fg, ab], writes=[lfcat])
          lfb = Buf()
          dma("pool", LFd.rearrange("(b p) h -> p b h", p=128), lfo.rearrange("p (b h) -> p b h", b=NB),
              reads=[lfcat], writes=[lfb])
          S.op("pool", lambda e: e.collective_compute("AllGather", ALU.bypass,
                                                      replica_groups=[[0, 1], [2, 3], [4, 5], [6, 7]],
                                                      ins=[LFd], outs=[LF2]), reads=[lfb], writes=[lfb], kind="cc")
          dma("pool", lfcat.ap[:, 0:NB * HA].rearrange("p (b h) -> p b h", b=NB),
              LF2[0:NT, :].rearrange("(b p) h -> p b h", p=128), reads=[lfb, lfcat], writes=[lfcat])
          ts("dve", lfcat.ap[:, 0:NB * HA], lfcat.ap[:, 0:NB * HA], sel.ap[:, 5:6], None, ALU.mult, reads=[lfcat, sel],
             writes=[lfcat])
          kvb = Buf()
          for j in range(WA // RCK):
              S.op("pool", lambda e, j=j: e.collective_compute(
                  "AllGather", ALU.bypass, replica_groups=[[0, 1], [2, 3], [4, 5], [6, 7]],
                  ins=[KNT[j * RCK:(j + 1) * RCK, :]], outs=[KN2[2 * j * RCK:(2 * j + 2) * RCK, :]]),
                  reads=[scr], appends=[kvb], kind="cc")
          for j in range(NT // RCV):
              S.op("pool", lambda e, j=j: e.collective_compute(
                  "AllGather", ALU.bypass, replica_groups=[[0, 1], [2, 3], [4, 5], [6, 7]],
                  ins=[VAd[j * RCV:(j + 1) * RCV, :]], outs=[VA2[2 * j * RCV:(2 * j + 2) * RCV, :]]),
                  reads=[scr], appends=[kvb], kind="cc")
          Fc = A.alloc(2 * NB * HA, F32)
          lfv = lfcat.ap.rearrange("p (b h) -> p b h", b=2 * NB)
          for b in range(2 * NB):
              ps = PS[2 + b % 2]
              for b2 in range(b):
                  mm(ps.ap[:, 0:HA], ones, lfv[:, b2, :], b2 == 0, False, reads=[lfcat, consts], writes=[ps])
              mm(ps.ap[:, 0:HA], tri, lfv[:, b, :], b == 0, True, reads=[lfcat, consts], writes=[ps])
              cp("dve", Fc.ap[:, b * HA:(b + 1) * HA], ps.ap[:, 0:HA], reads=[ps], appends=[Fc])
          nb_ = A.alloc(2 * NB * HA, F32)
          ts("dve", nb_.ap, Fc.ap, -1.0, Mb.ap[:, 2 * l + 1:2 * l + 2], ALU.mult, ALU.add, reads=[Fc, Mb], writes=[nb_])
          ts("dve", nb_.ap[:, 0:NB * HA], nb_.ap[:, 0:NB * HA], sel.ap[:, 4:5], None, ALU.add, reads=[nb_, sel],
             writes=[nb_])
          Frow = A.alloc(NT, F32, parts=16)
          for b in range(NB):
              ps = PS[4 + b % 2]
              tr(ps.ap[0:HA, 0:128], Fc.ap[:, (NB + b) * HA:(NB + b + 1) * HA], ident, reads=[Fc, consts], writes=[ps])
              cp("dve", Frow.ap[0:HA, b * 128:(b + 1) * 128], ps.ap[0:HA, 0:128], reads=[ps], appends=[Frow])

          chk("B%d" % l)
          S.barrier()
          kTp = [A.alloc(NT, BF16) for _ in range(2)]
          kTo = [A.alloc(NT, BF16) for _ in range(2)]
          vp = [A.alloc(NB * 128, BF16) for _ in range(2)]
          vo = [A.alloc(NB * 128, BF16) for _ in range(2)]
          qn = [A.alloc(TT, BF16) for _ in range(2)]
          Fqb = [A.alloc(TT, F32) for _ in range(2)]
          stmp = [A.alloc(TT, F32) for _ in range(2)]
          PT = [A.alloc(TT, BF16) for _ in range(3)]
          rinv = A.alloc(TT, F32)
          oat = [A.alloc(TT, BF16) for _ in range(2)]
          bi = 0
          qi = 0
          for h in range(HA):
              hb = h % 2
              kj, ko = (h * 128) // RCK, (h * 128) % RCK
              dma("sp", kTp[hb].ap, KN2[2 * kj * RCK + ko:2 * kj * RCK + ko + 128, :], reads=[kvb], writes=[kTp[hb]])
              dma("sp", kTo[hb].ap, KNT[h * 128:(h + 1) * 128, :], reads=[scr], writes=[kTo[hb]])
              for vj in range(NT // RCV):
                  q_ = RCV // 128
                  dma("sp", vp[hb].ap[:, vj * RCV:(vj + 1) * RCV].rearrange("p (b d) -> p b d", b=q_),
                      VA2[2 * vj * RCV:2 * vj * RCV + RCV, h * 128:(h + 1) * 128].rearrange("(b p) d -> p b d", p=128),
                      reads=[kvb], appends=[vp[hb]])
              dma("sp", vo[hb].ap.rearrange("p (b d) -> p b d", b=NB),
                  VAd[:, h * 128:(h + 1) * 128].rearrange("(b p) d -> p b d", p=128), reads=[scr], writes=[vo[hb]])
              for j in range(NTT):
                  t0 = j * TT
                  q_ = qn[qi % 2]
                  fq = Fqb[qi % 2]
                  psO, psR = PS[4 + qi % 2 * 0], PS[5]
                  psO = PS[4]
                  qi += 1
                  dma("sp", q_.ap, QNT[h * 128:(h + 1) * 128, t0:t0 + TT], reads=[scr], writes=[q_])
                  mm(PS[6].ap, selb.ap[0:HA, h * 128:(h + 1) * 128], Frow.ap[0:HA, t0:t0 + TT], True, True,
                     reads=[selb, Frow], writes=[PS[6]])
                  cp("dve", fq.ap, PS[6].ap, reads=[PS[6]], writes=[fq])
                  blocks = [(0, kb) for kb in range(NB)] + [(1, kb) for kb in range(4 * j + 4)]
                  for bidx, (own, kb) in enumerate(blocks):
                      last = bidx == len(blocks) - 1
                      kT = (kTo if own else kTp)[hb]
                      v = (vo if own else vp)[hb]
                      dI = kb - 4 * j if own else -1
                      c0 = 128 * dI if dI > 0 else 0
                      psS = PS[bi % 2]
                      st = stmp[bi % 2]
                      p_ = PT[bi % 3]
                      bi += 1
                      mm(psS.ap[:, c0:TT], kT.ap[:, kb * 128:(kb + 1) * 128], q_.ap[:, c0:TT], True, True,
                         reads=[kT, q_], writes=[psS])
                      tt("dve", st.ap[:, c0:TT], psS.ap[:, c0:TT], fq.ap[:, c0:TT], ALU.add, reads=[psS, fq], writes=[st])
                      if dI >= 0:
                          tt("dve", st.ap[:, c0:c0 + 128], st.ap[:, c0:c0 + 128], negm.ap, ALU.add, reads=[st, negm],
                             writes=[st])
                      col = ((NB if own else 0) + kb) * HA + h
                      act(p_.ap[:, c0:TT], st.ap[:, c0:TT], AF.Exp, reads=[st, nb_], writes=[p_], scale=scale,
                          bias=nb_.ap[:, col:col + 1])
                      mm(psO.ap[:, c0:TT], v.ap[:, kb * 128:(kb + 1) * 128], p_.ap[:, c0:TT], bidx == 0, last,
                         reads=[v, p_], writes=[psO])
                      mm(psR.ap[:, c0:TT], onesb, p_.ap[:, c0:TT], bidx == 0, last, reads=[p_, cb16], writes=[psR])
                  S.op("dve", lambda e, psR=psR: e.reciprocal(out=rinv.ap, in_=psR.ap), reads=[psR], writes=[rinv])
                  o_ = oat[qi % 2]
                  tt("dve", o_.ap, psO.ap, rinv.ap, ALU.mult, reads=[psO, rinv], writes=[o_])
                  dma("pool", MIXT[WR + h * 128:WR + (h + 1) * 128, t0:t0 + TT], o_.ap, reads=[o_], appends=[scr])

          chk("C%d" % l)
          phase()
          NG = NT // 512
          CS = A.alloc(HR * NCH * 3, F32)
          csv = CS.ap.rearrange("p (h c k) -> p h c k", h=HR, c=NCH)
          zt = [A.alloc(512, F32) for _ in range(2)]
          qs = [A.alloc(512, BF16) for _ in range(2)]
          kk = [A.alloc(512, F32) for _ in range(2)]
          lf = [A.alloc(512, F32) for _ in range(2)]
          Gt = [A.alloc(512, F32) for _ in range(2)]
          E1 = [A.alloc(512, F32) for _ in range(2)]
          qg = [A.alloc(512, BF16) for _ in range(2)]
          kg = [A.alloc(512, BF16) for _ in range(2)]
          dlrs = [A.alloc(NC5, F32) for _ in range(2)]
          i = 0
          for h in range(HR):
              for g in range(NG):
                  t0 = g * 512
                  z_, q_, k_, l_, G_, E_, qg_, kg_ = (zt[i % 2], qs[i % 2], kk[i % 2], lf[i % 2], Gt[i % 2], E1[i % 2],
                                                      qg[i % 2], kg[i % 2])
                  i += 1
                  dma("sp", z_.ap, ZRT[h * 128:(h + 1) * 128, t0:t0 + 512], reads=[scr], writes=[z_])
                  dma("sp", q_.ap, QRT[h * 128:(h + 1) * 128, t0:t0 + 512], reads=[scr], writes=[q_])
                  act(z_.ap, z_.ap, AF.Sigmoid, reads=[z_], writes=[z_], scale=-1.0)
                  ts("dve", k_.ap, z_.ap, omlb.ap[:, l * HR + h:l * HR + h + 1], K_MAX, ALU.mult, ALU.min,
                     reads=[z_, omlb], writes=[k_])
                  act(l_.ap, k_.ap, AF.Ln, reads=[k_], writes=[l_], scale=-1.0, bias=1.0)
                  for cchunk in range(NC5):
                      sl = slice(cchunk * CH, (cchunk + 1) * CH)
                      S.op("dve", lambda e, G_=G_, l_=l_, sl=sl: e.tensor_tensor_scan(
                          out=G_.ap[:, sl], data0=ones[:, 0:CH], data1=l_.ap[:, sl], initial=0.0, op0=ALU.mult,
                          op1=ALU.add), reads=[l_, consts], writes=[G_])
                  Gv = G_.ap.rearrange("p (c t) -> p c t", t=CH)
                  cidx = g * NC5
                  act(csv[:, h, cidx:cidx + NC5, 0], Gv[:, :, CH - 1], AF.Exp, reads=[G_], appends=[CS])
                  act(csv[:, h, cidx:cidx + NC5, 2], Gv[:, :, CH // 2 - 1], AF.Exp, reads=[G_], appends=[CS])
                  dlr = dlrs[i % 2]
                  tt("dve", dlr.ap, Gv[:, :, CH - 1], Gv[:, :, CH // 2 - 1], ALU.subtract, reads=[G_], writes=[dlr])
                  act(csv[:, h, cidx:cidx + NC5, 1], dlr.ap, AF.Exp, reads=[dlr], appends=[CS])
                  for cchunk in range(NC5):
                      sl = slice(cchunk * CH, (cchunk + 1) * CH)
                      ts("dve", l_.ap[:, sl], G_.ap[:, sl], Gv[:, cchunk, CH // 2 - 1:CH // 2], None, ALU.subtract, reads=[G_, l_],
                         writes=[l_])
                  act(E_.ap, l_.ap, AF.Exp, reads=[l_], writes=[E_])
                  tt("dve", qg_.ap, q_.ap, E_.ap, ALU.mult, reads=[q_, E_], writes=[qg_])
                  act(E_.ap, l_.ap, AF.Exp, reads=[l_, qg_], writes=[E_], scale=-1.0)
                  tt("dve", kg_.ap, k_.ap, E_.ap, ALU.mult, reads=[k_, E_], writes=[kg_])
                  dma("pool", QGT[h * 128:(h + 1) * 128, t0:t0 + 512], qg_.ap, reads=[qg_], appends=[scr])
                  dma("pool", KGT[h * 128:(h + 1) * 128, t0:t0 + 512], kg_.ap, reads=[kg_], appends=[scr])

          chk("D1%d" % l)
          S.barrier()
          SS = A.alloc(HR * 128, F32)
          S.op("dve", lambda e: e.memset(SS.ap, 0.0), writes=[SS])
          kgl = [A.alloc(HR * GS, BF16) for _ in range(2)]
          qgl = [A.alloc(HR * GS, BF16) for _ in range(2)]
          vl = [A.alloc(NCG * WR, BF16, parts=CH) for _ in range(2)]
          kgtm = [A.alloc(128, BF16, parts=CH) for _ in range(4)]
          utmp = [A.alloc(128, F32) for _ in range(4)]
          Sb = [A.alloc(128, BF16) for _ in range(4)]
          ATm = [A.alloc(CH, BF16, parts=CH) for _ in range(4)]
          OTs = A.alloc(HR * GS, F32)
          sqo = [A.alloc(GS, F32) for _ in range(2)]
          rso = [A.alloc(GS, F32) for _ in range(2)]
          grl = [A.alloc(GS, BF16) for _ in range(2)]
          yo = [A.alloc(GS, BF16) for _ in range(2)]
          sendb = Buf()
          ui = 0
          for pss_ in range(2):
              emit = pss_ == 1
              for g in range(NT // GS):
                  t0 = g * GS
                  kgl_, qgl_, vl_ = kgl[g % 2], qgl[g % 2], vl[g % 2]
                  dma("sp", kgl_.ap.rearrange("p (h t) -> p h t", h=HR),
                      KGT[:, t0:t0 + GS].rearrange("(h p) t -> p h t", p=128), reads=[scr], writes=[kgl_])
                  dma("sp", vl_.ap.rearrange("p (c w) -> p c w", c=NCG),
                      IRd[t0:t0 + GS, :].rearrange("(c p) w -> p c w", p=CH), reads=[scr], writes=[vl_])
                  if emit:
                      dma("sp", qgl_.ap.rearrange("p (h t) -> p h t", h=HR),
                          QGT[:, t0:t0 + GS].rearrange("(h p) t -> p h t", p=128), reads=[scr], writes=[qgl_])
                  for cchunk in range(NCG):
                      cg = g * NCG + cchunk
                      for h in range(HR):
                          u = ui % 4
                          ui += 1
                          kgc = kgl_.ap[:, h * GS + cchunk * CH:h * GS + (cchunk + 1) * CH]
                          vc = vl_.ap[:, cchunk * WR + h * 128:cchunk * WR + (h + 1) * 128]
                          Sh = SS.ap[:, h * 128:(h + 1) * 128]
                          if emit:
                              qgc = qgl_.ap[:, h * GS + cchunk * CH:h * GS + (cchunk + 1) * CH]
                              act(Sb[u].ap, Sh, AF.Identity, reads=[SS, CS], writes=[Sb[u]], scale=csv[:, h, cg, 2:3])
                              psA = PS[u % 2]
                              mm(psA.ap[0:CH, 0:CH], kgc, qgc, True, True, reads=[kgl_, qgl_], writes=[psA])
                              tt("dve", ATm[u].ap, psA.ap[0:CH, 0:CH], tri[0:CH, 0:CH], ALU.mult, reads=[psA, consts],
                                 writes=[ATm[u]])
                              psO = PS[2 + u % 2]
                              mm(psO.ap[:, 0:CH], vc, ATm[u].ap, True, False, reads=[vl_, ATm[u]], writes=[psO])
                              mm(psO.ap[:, 0:CH], Sb[u].ap, qgc, False, True, reads=[Sb[u], qgl_], writes=[psO])
                              cp("act", OTs.ap[:, h * GS + cchunk * CH:h * GS + (cchunk + 1) * CH], psO.ap[:, 0:CH],
                                 reads=[psO], appends=[OTs])
                          tr(PSB.ap[0:CH, (u % 2) * 128:(u % 2) * 128 + 128], kgc, identb, reads=[kgl_, cb16],
                             writes=[PSB])
                          cp("act", kgtm[u].ap, PSB.ap[0:CH, (u % 2) * 128:(u % 2) * 128 + 128], reads=[PSB],
                             writes=[kgtm[u]])
                          psU = PS[4 + u % 2]
                          mm(psU.ap[:, 0:128], kgtm[u].ap, vc, True, True, reads=[kgtm[u], vl_], writes=[psU])
                          ts("dve", utmp[u].ap, psU.ap[:, 0:128], csv[:, h, cg, 1:2], None, ALU.mult, reads=[psU, CS],
                             writes=[utmp[u]])
                          stt(Sh, Sh, csv[:, h, cg, 0:1], utmp[u].ap, ALU.mult, ALU.add, reads=[SS, utmp[u], CS],
                              writes=[SS])
                  if emit:
                      for h in range(HR):
                          s_, r_, g_, y_ = sqo[h % 2], rso[h % 2], grl[h % 2], yo[h % 2]
                          oh = OTs.ap[:, h * GS:(h + 1) * GS]
                          dma("sp", g_.ap, GRT[h * 128:(h + 1) * 128, t0:t0 + GS], reads=[scr], writes=[g_])
                          act(s_.ap, oh, AF.Square, reads=[OTs], writes=[s_])
                          mm(PS[6].ap[:, 0:GS], ones, s_.ap, True, True, reads=[s_, consts], writes=[PS[6]])
                          ts("dve", r_.ap, PS[6].ap[:, 0:GS], 1.0 / 128, EPS, ALU.mult, ALU.add, reads=[PS[6]], writes=[r_])
                          rsq(r_)
                          tt("dve", r_.ap, r_.ap, oh, ALU.mult, reads=[r_, OTs], writes=[r_])
                          stt(y_.ap, r_.ap, rnwT.ap[:, l:l + 1], g_.ap, ALU.mult, ALU.mult, reads=[r_, g_, rnwT],
                              writes=[y_])
                          dma("pool", MIXT[h * 128:(h + 1) * 128, t0:t0 + GS], y_.ap, reads=[y_], appends=[scr])
              if not emit:
                  dma("pool", SEND.rearrange("(h p) v -> p h v", p=128), SS.ap.rearrange("p (h v) -> p h v", h=HR),
                      reads=[SS], writes=[sendb])
                  S.op("pool", lambda e: e.collective_compute("AllGather", ALU.bypass,
                                                              replica_groups=[[0, 1], [2, 3], [4, 5], [6, 7]],
                                                              ins=[SEND], outs=[SGAT]), reads=[sendb], writes=[sendb],
                       kind="cc")
                  dma("pool", SS.ap.rearrange("p (h v) -> p h v", h=HR),
                      SGAT[0:WR, :].rearrange("(h p) v -> p h v", p=128), reads=[sendb, SS], writes=[SS])
                  ts("dve", SS.ap, SS.ap, sel.ap[:, 5:6], None, ALU.mult, reads=[SS, sel], writes=[SS])

          chk("D%d" % l)
          phase()
          Wout = wfull[("out", l)]
          mT = A.alloc(KM * TT, BF16)
          wt = [A.alloc(KM * 256, BF16) for _ in range(3)]
          xres = [A.alloc(TT, F32) for _ in range(3)]
          xo = [A.alloc(TT, F32) for _ in range(3)]
          wi = 0
          oi = 0
          g1 = mod(l, 2)
          for ttile in range(NTT):
              t0 = ttile * TT
              dma("sp", mT.ap.rearrange("p (k t) -> p k t", k=KM),
                  MIXT[:, t0:t0 + TT].rearrange("(k p) t -> p k t", p=128), reads=[scr], writes=[mT])
              for cb in range(0, D, 256):
                  w = wt[wi % 3]
                  wi += 1
                  dma("sp", w.ap.rearrange("p (k n) -> p k n", k=KM),
                      Wout[:, cb:cb + 256].rearrange("(k p) n -> p k n", p=128), reads=[wb[("out", l)]], writes=[w])
                  for hb in range(2):
                      nb0 = (cb + hb * 128) // 128
                      ps = PS[oi % 4]
                      xr_, xo_ = xres[oi % 3], xo[oi % 3]
                      oi += 1
                      dma("sp", xr_.ap, Xa[nb0 * 128:(nb0 + 1) * 128, t0:t0 + TT], reads=[xt_b], writes=[xr_])
                      for kc in range(KM):
                          mm(ps.ap, w.ap[:, kc * 256 + hb * 128:kc * 256 + (hb + 1) * 128],
                             mT.ap[:, kc * TT:(kc + 1) * TT], kc == 0, kc == KM - 1, reads=[w, mT], writes=[ps])
                      stt(xo_.ap, ps.ap, g1[:, nb0:nb0 + 1], xr_.ap, ALU.mult, ALU.add, reads=[ps, xr_, modT],
                          writes=[xo_])
                      dma("pool", Xb[nb0 * 128:(nb0 + 1) * 128, t0:t0 + TT], xo_.ap, reads=[xo_], appends=[xt_b])

          chk("E%d" % l)
          phase()
          Wup, Wdn = wfull[("up", l)], wfull[("dn", l)]
          hreg = A.alloc(max(KC * TF // 2, 5 * TF), F32)
          hT = Tile(hreg.ap.bitcast(BF16)[:, 0:KC * TF])
          al = [hreg.ap[:, i * TF:(i + 1) * TF] for i in range(5)]
          uT = A.alloc(FC * TF, BF16)
          xr = [A.alloc(TF, F32) for _ in range(2)]
          GF = min(8, FC)
          UW = 128
          wt = [A.alloc(max(KC * UW, GF * 512), BF16) for _ in range(3)]
          rtmp = [A.alloc(TF, F32) for _ in range(1)]
          g2 = mod(l, 5)
          ptm = prenorm_tmps(TF, 1)
          wi = 0
          oi = 0
          for ttile in range(NT // TF):
              t0 = ttile * TF
              prenorm(l, 1, Xb, t0, hT, xr, TF, ptm, G4=1)
              for cb in range(0, DFF, UW):
                  w = wt[wi % 3]
                  wi += 1
                  dma("sp", w.ap[:, 0:KC * UW].rearrange("p (k n) -> p k n", k=KC),
                      Wup[:, cb:cb + UW].rearrange("(k p) n -> p k n", p=128), reads=[wb[("up", l)]], writes=[w])
                  fc = cb // 128
                  ps = PS[oi % 2]
                  r_ = rtmp[0]
                  oi += 1
                  for kc in range(KC):
                      mm(ps.ap[:, 0:TF], w.ap[:, kc * UW:(kc + 1) * UW],
                         hT.ap[:, kc * TF:(kc + 1) * TF], kc == 0, kc == KC - 1, reads=[w, hT], writes=[ps])
                  act(r_.ap, ps.ap[:, 0:TF], AF.Relu, reads=[ps], writes=[r_])
                  tt("dve", uT.ap[:, fc * TF:(fc + 1) * TF], r_.ap, r_.ap, ALU.mult, reads=[r_], appends=[uT])
              for n0 in range(0, D, 512):
                  for fg_ in range(0, FC, GF):
                      w = wt[wi % 3]
                      wi += 1
                      dma("sp", w.ap[:, 0:GF * 512].rearrange("p (k n) -> p k n", k=GF),
                          Wdn[fg_ * 128:(fg_ + GF) * 128, n0:n0 + 512].rearrange("(k p) n -> p k n", p=128),
                          reads=[wb[("dn", l)]], writes=[w])
                      for f in range(GF):
                          fc = fg_ + f
                          for nb in range(4):
                              mm(PS[2 + nb].ap[:, 0:TF], w.ap[:, f * 512 + nb * 128:f * 512 + (nb + 1) * 128],
                                 uT.ap[:, fc * TF:(fc + 1) * TF], fc == 0, fc == FC - 1, reads=[w, uT],
                                 writes=[PS[2 + nb]])
                  for nb in range(4):
                      nb0 = n0 // 128 + nb
                      xr_, xo_ = al[nb % 2], al[2 + nb % 2]
                      dma("sp", xr_, Xb[nb0 * 128:(nb0 + 1) * 128, t0:t0 + TF], reads=[xt_b], appends=[hT])
                      stt(xo_, PS[2 + nb].ap[:, 0:TF], g2[:, nb0:nb0 + 1], xr_, ALU.mult, ALU.add,
                          reads=[PS[2 + nb], hT, modT], appends=[hT])
                      if l == 0:
                          dma("pool", Xa[nb0 * 128:(nb0 + 1) * 128, t0:t0 + TF], xo_, reads=[hT], appends=[xt_b])
                      else:
                          ys = al[4]
                          for tsb in range(TF // 128):
                              tr(PS[6].ap[:, tsb * 128:(tsb + 1) * 128], xo_[:, tsb * 128:(tsb + 1) * 128], ident,
                                 reads=[hT, consts], writes=[PS[6]])
                          cp("act", ys, PS[6].ap[:, 0:TF], reads=[PS[6]], appends=[hT])
                          dma("pool", y_out[t0:t0 + TF, nb0 * 128:(nb0 + 1) * 128].rearrange("(s p) n -> p s n", p=128),
                              ys.rearrange("p (s n) -> p s n", s=TF // 128), reads=[hT], appends=[xt_b])
    S.barrier()
    S.emit(nc, stack)
    stack.close()
    return nc


def make_consts():
    c = np.zeros((128, 384), np.float32)
    c[:, 0:128] = np.eye(128, dtype=np.float32)
    c[:, 128:256] = np.triu(np.ones((128, 128), np.float32))
    c[:, 256:384] = 1.0
    sb = np.zeros((16, 16, 128), np.float32)
    for h in range(16):
        sb[h, h, :] = math.sqrt(128.0)
    return c, sb.reshape(16, 2048)


def make_in_maps(cfg, x, c, lower_bounds, w_ada, b_ada, norm_mix_w, norm_ffn_w, w_in, rec_norm_w, fg_bias,
                 q_norm_w, k_norm_w, w_out, w_up, w_down):
    f = lambda a: np.ascontiguousarray(np.asarray(a, dtype=np.float32))
    D, NT, KC, DR = cfg.D, cfg.NT, cfg.KC, cfg.DR
    consts, selb = make_consts()
    x, c, w_ada, w_in, w_out, w_up, w_down = f(x), f(c), f(w_ada), f(w_in), f(w_out), f(w_up), f(w_down)
    cT = np.ascontiguousarray(c.T)
    maps = []
    AGLIM = cfg.AGLIM

    def p2floor(v):
        p = 1
        while p * 2 <= v:
            p *= 2
        return p

    def bc(w, R, r):
        L, rows, C = w.shape
        rc = min(R, p2floor(AGLIM // (cfg.WS * C * 2)))
        return np.ascontiguousarray(w.reshape(L, rows // (cfg.WS * rc), cfg.WS, rc, C)[:, :, r].reshape(L, R, C))

    for core in range(NCORES):
        b, s = core // 2, core % 2
        sel = np.zeros((128, 8), np.float32)
        sel[:, b] = 1.0
        sel[:, 4] = 0.0 if s == 1 else -1e30
        sel[:, 5] = 1.0 if s == 1 else 0.0
        r = core % cfg.WS
        mo = cfg.MIXW // cfg.WS
        fo = cfg.DFF // cfg.WS
        maps.append({
            "x": f(x[b, s * NT:(s + 1) * NT, :]),
            "cT": f(cT[r * DR:(r + 1) * DR, :]),
            "w_ada": f(w_ada[:, r * DR:(r + 1) * DR, :]),
            "b_ada": f(b_ada).reshape(12 * KC, 128),
            "norm_mix_w": f(norm_mix_w).reshape(2 * KC, 128),
            "norm_ffn_w": f(norm_ffn_w).reshape(2 * KC, 128),
            "lower_bounds": f(lower_bounds).reshape(2 * cfg.HR, 128),
            "rec_norm_w": f(rec_norm_w), "q_norm_w": f(q_norm_w), "k_norm_w": f(k_norm_w),
            "fg_bias": f(fg_bias).reshape(1, 2 * cfg.HA),
            "w_in": bc(w_in, DR, r), "w_out": bc(w_out, mo, r), "w_up": bc(w_up, DR, r), "w_down": bc(w_down, fo, r),
            "consts": consts, "selb": selb, "sel": sel,
        })
    return maps


def run(cfg, **inputs):
    nc = build(cfg)
    maps = make_in_maps(cfg, **inputs)
    res = run_bass_kernel_spmd(nc, maps, core_ids=list(range(NCORES)))
    out = np.zeros((cfg.B, cfg.SEQ, cfg.D), np.float32)
    for core in range(NCORES):
        b, s = core // 2, core % 2
        out[b, s * cfg.NT:(s + 1) * cfg.NT, :] = res.results[core]["y"]
    return out


def kernel(**inputs):
    return run(Cfg(), **inputs)
```

```python
import math
import numpy as np
import concourse.bass as bass
import concourse.mybir as mybir
from concourse.bass_utils import run_bass_kernel_spmd

F32, BF16 = mybir.dt.float32, mybir.dt.bfloat16
AF = mybir.ActivationFunctionType
ALU = mybir.AluOpType
EPS = 1e-6
K_MAX = 1.0 - 1e-6
NCORES = 8
ARENA_BYTES = 204 * 1024


class Cfg:
    def __init__(self, D=4096, SEQ=4096, B=4, DFF=None):
        self.D, self.SEQ, self.B = D, SEQ, B
        self.NT = SEQ // 2
        self.KC = D // 128
        self.NH = D // 128
        self.HR = self.NH // 2
        self.HA = self.NH - self.HR
        self.WR, self.WA = self.HR * 128, self.HA * 128
        self.MIXW = self.WR + self.WA
        self.KM = self.MIXW // 128
        self.DFF = DFF or 4 * D
        self.FC = self.DFF // 128
        self.IC = 4 * self.WR + 3 * self.WA + self.HA
        self.WS = 4
        self.AGLIM = 4 * 1024 * 1024
        self.DR = D // 4
        self.TT = 512
        self.TF = 512
        self.NTT = self.NT // 512
        self.NB = self.NT // 128
        self.CH = 32
        self.NCH = self.NT // self.CH
        self.stop = None
        self.dump = None


class Op:
    __slots__ = ("eng", "fn", "deps", "signal", "sem", "cnt", "kind", "idx")


class Buf:
    REG = []

    def __init__(self):
        self.writers, self.readers = [], []
        Buf.REG.append(self)


class Tile(Buf):
    def __init__(self, ap):
        Buf.__init__(self)
        self.ap = ap


class Sched:
    def __init__(self):
        self.ops = []
        self.ring = {"sp": 24, "pool": 16}
        self.ring_i = {"sp": 0, "pool": 0}
        self.ring_last = {}
        self.last = {}
        self.stopped = False

    def op(self, eng, fn, reads=(), writes=(), appends=(), kind="c"):
        if self.stopped:
            return None
        o = Op()
        o.eng, o.fn, o.kind, o.signal, o.sem, o.cnt = eng, fn, kind, False, None, 0
        deps = []
        for b in reads:
            deps += b.writers
        for b in writes:
            deps += b.writers + b.readers
        for b in appends:
            deps += b.readers
        if kind == "d":
            slot = (eng, self.ring_i[eng] % self.ring[eng])
            self.ring_i[eng] += 1
            if slot in self.ring_last:
                deps.append(self.ring_last[slot])
            self.ring_last[slot] = o
            o.sem = slot
            o.signal = True
        elif kind == "cc":
            o.sem = ("cc", 0)
            o.signal = True
        else:
            o.sem = (eng, -1)
        best = {}
        for d in deps:
            if d.eng == "pe" and eng == "pe" and d.kind == "c" and kind == "c":
                continue
            p = best.get(d.sem)
            if p is None or p.idx < d.idx:
                best[d.sem] = d
        dl = list(best.values())
        for d in dl:
            d.signal = True
        o.deps = dl
        o.idx = len(self.ops)

        def add(lst, o):
            lst[:] = [x for x in lst if x.sem != o.sem]
            lst.append(o)

        for b in reads:
            add(b.readers, o)
        for b in writes:
            b.writers = [o]
            b.readers = []
        for b in appends:
            add(b.writers, o)
        self.ops.append(o)
        if kind == "c" or kind == "cc":
            self.last[o.sem] = o
        return o

    def barrier(self):
        if self.stopped:
            return
        tails = list(self.last.values()) + list(self.ring_last.values())
        for e in ("sp", "pe", "act", "dve", "pool"):
            o = Op()
            o.eng, o.fn, o.kind, o.signal, o.sem, o.cnt = e, None, "b", False, None, 0
            o.idx = len(self.ops)
            o.deps = list(tails)
            for d in tails:
                d.signal = True
            self.ops.append(o)
        for b in Buf.REG:
            b.writers, b.readers = [], []

    def emit(self, nc, stack):
        sems = {}

        def getsem(key):
            if key not in sems:
                sems[key] = stack.enter_context(nc.semaphore("s_%s_%d" % (key[0], key[1] + 1)))
            return sems[key]

        counts = {}
        for o in self.ops:
            if o.signal:
                inc = 16 if o.kind == "d" else 1
                counts[o.sem] = counts.get(o.sem, 0) + inc
                o.cnt = counts[o.sem]
                getsem(o.sem)
        block = stack.enter_context(nc.Block())
        engmap = {"sp": block.sync, "pe": block.tensor, "act": block.scalar, "dve": block.vector,
                  "pool": block.gpsimd}
        ops = self.ops
        for ename, deco in engmap.items():
            def run(e, ename=ename):
                waited = {}
                for o in ops:
                    if o.eng != ename:
                        continue
                    need = {}
                    for d in o.deps:
                        if need.get(d.sem, 0) < d.cnt:
                            need[d.sem] = d.cnt
                    for sk, c in need.items():
                        if waited.get(sk, 0) < c:
                            e.wait_ge(sems[sk], c)
                            waited[sk] = c
                    if o.fn is not None:
                        ins = o.fn(e)
                        if o.signal:
                            if o.kind == "d":
                                ins.then_inc(sems[o.sem], 16)
                            elif o.kind == "cc":
                                ins.then_inc(sems[o.sem])
                            else:
                                ins.then_inc(sems[o.sem], 1)
            deco(run)


class Arena:
    def __init__(self, big):
        self.big = big
        self.base = 0
        self.ptr = 0

    def alloc(self, cols, dt, parts=128):
        nb = cols * (4 if dt == F32 else 2)
        nb = (nb + 63) // 64 * 64
        assert self.ptr + nb <= ARENA_BYTES, ("arena overflow", self.ptr, nb)
        a = self.big[:, self.ptr // 4:(self.ptr + nb) // 4]
        self.ptr += nb
        if dt != F32:
            a = a.bitcast(dt)
        a = a[0:parts, 0:cols]
        return Tile(a)

    def persist(self):
        self.base = self.ptr

    def reset(self):
        self.ptr = self.base


def build(cfg):
    Buf.REG = []
    from contextlib import ExitStack
    c = cfg
    D, NT, KC, HR, HA, WR, WA, DFF, FC, IC, DR = c.D, c.NT, c.KC, c.HR, c.HA, c.WR, c.WA, c.DFF, c.FC, c.IC, c.DR
    MIXW, KM, TT, NTT, NB, NCH = c.MIXW, c.KM, c.TT, c.NTT, c.NB, c.NCH
    WS = c.WS
    AGLIM = c.AGLIM

    def p2floor(v):
        p = 1
        while p * 2 <= v:
            p *= 2
        return p

    def wchunk(r, cc_):
        return min(r, p2floor(AGLIM // (WS * cc_ * 2)))

    RCK = min(c.WA, p2floor(AGLIM // (2 * c.NT * 2)))
    RCV = min(c.NT, max(128, p2floor(AGLIM // (2 * c.WA * 2))))
    GS = 128
    NCG = GS // c.CH
    TF = c.TF
    CH = c.CH
    NC5 = 512 // CH
    G4 = [[0, 1, 2, 3], [4, 5, 6, 7]]
    nc = bass.Bass("TRN2", target_bir_lowering=False)
    stack = ExitStack()

    def din(name, shape, dt=F32):
        return nc.dram_tensor(name, list(shape), dt, kind="ExternalInput").ap()

    def dscr(name, shape, dt):
        return nc.dram_tensor(name, list(shape), dt).ap()

    x_in = din("x", [NT, D])
    cT_in = din("cT", [DR, 4])
    wada_in = din("w_ada", [2, DR, 6 * D])
    bada_in = din("b_ada", [12 * KC, 128])
    nwm_in = din("norm_mix_w", [2 * KC, 128])
    nwf_in = din("norm_ffn_w", [2 * KC, 128])
    lb_in = din("lower_bounds", [2 * HR, 128])
    rnw_in = din("rec_norm_w", [2, 128])
    qnw_in = din("q_norm_w", [2, 128])
    knw_in = din("k_norm_w", [2, 128])
    fgb_in = din("fg_bias", [1, 2 * HA])
    win_in = din("w_in", [2, DR, IC])
    wout_in = din("w_out", [2, MIXW // WS, D])
    wup_in = din("w_up", [2, DR, DFF])
    wdn_in = din("w_down", [2, DFF // WS, D])
    consts_in = din("consts", [128, 384])
    selb_in = din("selb", [16, 16 * 128])
    sel_in = din("sel", [128, 8])
    y_out = nc.dram_tensor("y", [NT, D], F32, kind="ExternalOutput").ap()

    wsh = {}
    wfull = {}
    wspec = {"in": (win_in, DR, IC), "out": (wout_in, MIXW // WS, D), "up": (wup_in, DR, DFF),
             "dn": (wdn_in, DFF // WS, D)}
    for l in range(2):
        for k, (_, r, cc_) in wspec.items():
            wsh[(k, l)] = dscr("wsh_%s%d" % (k, l), [r, cc_], BF16)
            wfull[(k, l)] = dscr("wfull_%s%d" % (k, l), [WS * r, cc_], BF16)
    NM = 2 * 6 * KC * 4
    pm_d = dscr("pm_d", [128, NM], F32)
    pmg_d = dscr("pmg_d", [WS * 128, NM], F32)
    XT = [dscr("XT%d" % i, [D, NT], F32) for i in range(2)]
    QRT = dscr("QRT", [WR, NT], BF16)
    ZRT = dscr("ZRT", [WR, NT], F32)
    GRT = dscr("GRT", [WR, NT], BF16)
    IRd = dscr("IRd", [NT, WR], BF16)
    QAT = dscr("QAT", [WA, NT], F32)
    KAT = dscr("KAT", [WA, NT], F32)
    VAd = dscr("VAd", [NT, WA], BF16)
    VA2 = dscr("VA2", [2 * NT, WA], BF16)
    FGd = dscr("FGd", [NT, HA], F32)
    LFd = dscr("LFd", [NT, HA], F32)
    LF2 = dscr("LF2", [2 * NT, HA], F32)
    QNT = dscr("QNT", [WA, NT], BF16)
    KNT = dscr("KNT", [WA, NT], BF16)
    KN2 = dscr("KN2", [2 * WA, NT], BF16)
    QGT = dscr("QGT", [WR, NT], BF16)
    KGT = dscr("KGT", [WR, NT], BF16)
    SEND = dscr("SEND", [WR, 128], F32)
    SGAT = dscr("SGAT", [2 * WR, 128], F32)
    MIXT = dscr("MIXT", [MIXW, NT], BF16)

    big = stack.enter_context(nc.sbuf_tensor("big", [128, ARENA_BYTES // 4], F32))
    PS = [Tile(stack.enter_context(nc.psum_tensor("ps%d" % i, [128, 512], F32))[:]) for i in range(7)]
    PSB = Tile(stack.enter_context(nc.psum_tensor("psb", [128, 1024], BF16))[:])
    A = Arena(big)
    S = Sched()

    def dma(eng, out, in_, reads=(), writes=(), appends=(), **kw):
        return S.op(eng, lambda e: e.dma_start(out=out, in_=in_, **kw), reads, writes, appends, kind="d")

    def mm(out, lhsT, rhs, start, stop, reads=(), writes=(), appends=()):
        return S.op("pe", lambda e: e.matmul(out, lhsT, rhs, start=start, stop=stop), reads, writes, appends)

    def tr(out, in_, ident, reads=(), writes=(), appends=()):
        return S.op("pe", lambda e: e.transpose(out, in_, ident), reads, writes, appends)

    def act(out, in_, func, reads=(), writes=(), bias=None, scale=None, appends=()):
        kw = {}
        if bias is not None:
            kw["bias"] = bias
        if scale is not None:
            kw["scale"] = scale
        return S.op("act", lambda e: e.activation(out=out, in_=in_, func=func, **kw), reads, writes, appends)

    def ts(eng, out, in0, s1, s2, op0, op1=None, reads=(), writes=(), appends=()):
        if op1 is None:
            return S.op(eng, lambda e: e.tensor_scalar(out=out, in0=in0, scalar1=s1, scalar2=None, op0=op0),
                        reads, writes, appends)
        return S.op(eng, lambda e: e.tensor_scalar(out=out, in0=in0, scalar1=s1, scalar2=s2, op0=op0, op1=op1),
                    reads, writes, appends)

    def tt(eng, out, in0, in1, op, reads=(), writes=(), appends=()):
        return S.op(eng, lambda e: e.tensor_tensor(out=out, in0=in0, in1=in1, op=op), reads, writes, appends)

    def stt(out, in0, scalar, in1, op0, op1, reads=(), writes=(), appends=()):
        return S.op("dve", lambda e: e.scalar_tensor_tensor(out=out, in0=in0, scalar=scalar, in1=in1,
                                                            op0=op0, op1=op1), reads, writes, appends)

    def cp(eng, out, in_, reads=(), writes=(), appends=()):
        if eng == "act":
            return act(out, in_, AF.Identity, reads, writes, appends=appends)
        return S.op(eng, lambda e: e.tensor_copy(out=out, in_=in_), reads, writes, appends)

    def rsq(T):
        act(T.ap, T.ap, AF.Sqrt, reads=[T], writes=[T])
        S.op("dve", lambda e: e.reciprocal(out=T.ap, in_=T.ap), reads=[T], writes=[T])

    def phase():
        S.barrier()
        A.reset()

    def chk(name):
        if c.stop == name and not S.stopped:
            S.barrier()
            src = scr_names[c.dump]
            r, cc_ = src.shape
            r = min(r, NT)
            cc_ = min(cc_, D)
            dma("pool", y_out[0:r, 0:cc_], src[0:r, 0:cc_], max_dma_last_dim=2048)
            S.barrier()
            S.stopped = True

    scr_names = dict(XT0=XT[0], XT1=XT[1], QRT=QRT, ZRT=ZRT, GRT=GRT, IRd=IRd, QAT=QAT, KAT=KAT, VAd=VAd, VA2=VA2, FGd=FGd,
                     LFd=LFd, LF2=LF2, QNT=QNT, KNT=KNT, KN2=KN2, QGT=QGT, KGT=KGT, SEND=SEND, SGAT=SGAT, MIXT=MIXT,
                     pm_d=pm_d, pmg_d=pmg_d, wfull_in0=wfull[("in", 0)])

    consts = A.alloc(384, F32)
    ident, tri, ones = consts.ap[:, 0:128], consts.ap[:, 128:256], consts.ap[:, 256:384]
    cb16 = A.alloc(384, BF16)
    identb, trib, onesb = cb16.ap[:, 0:128], cb16.ap[:, 128:256], cb16.ap[:, 256:384]
    sel = A.alloc(8, F32)
    modT = A.alloc(12 * KC, F32)
    gsc = A.alloc(4 * KC, F32)
    nwT = A.alloc(4 * KC, F32)
    lbT = A.alloc(2 * HR, F32)
    omlb = A.alloc(2 * HR, F32)
    rnwT = A.alloc(2, F32)
    qnwT = A.alloc(2, F32)
    knwT = A.alloc(2, F32)
    fgb = A.alloc(2 * HA, F32)
    Mb = A.alloc(4, F32)
    onesrow = A.alloc(64, F32)
    negm = A.alloc(128, F32)
    A.persist()

    def mod(l, j):
        return modT.ap[:, (l * 6 + j) * KC:(l * 6 + j + 1) * KC]

    dma("sp", consts.ap, consts_in, writes=[consts])
    dma("sp", sel.ap, sel_in, writes=[sel])
    cp("dve", cb16.ap, consts.ap, reads=[consts], writes=[cb16])
    dma("sp", fgb.ap, fgb_in[0, :].partition_broadcast(128), writes=[fgb])
    cp("dve", onesrow.ap, consts.ap[:, 256:320], reads=[consts], writes=[onesrow])
    ts("dve", negm.ap, consts.ap[:, 128:256], -1.0, 1e30, ALU.add, ALU.mult, reads=[consts], writes=[negm])

    def load_T(dst_tile, dst_ap, src, R):
        r0 = 0
        while r0 < R:
            r = min(128, R - r0)
            t = A.alloc(128, F32)
            dma("sp", t.ap[0:r, :], src[r0:r0 + r, :], writes=[t])
            tr(PS[0].ap[:, 0:r], t.ap[0:r, :], ident[0:r, 0:r], reads=[t, consts], writes=[PS[0]])
            cp("dve", dst_ap[:, r0:r0 + r], PS[0].ap[:, 0:r], reads=[PS[0]], appends=[dst_tile])
            r0 += r

    badaT = A.alloc(12 * KC, F32)
    load_T(badaT, badaT.ap, bada_in, 12 * KC)
    load_T(nwT, nwT.ap[:, 0:2 * KC], nwm_in, 2 * KC)
    load_T(nwT, nwT.ap[:, 2 * KC:4 * KC], nwf_in, 2 * KC)
    load_T(lbT, lbT.ap, lb_in, 2 * HR)
    load_T(rnwT, rnwT.ap, rnw_in, 2)
    load_T(qnwT, qnwT.ap, qnw_in, 2)
    load_T(knwT, knwT.ap, knw_in, 2)
    S.op("dve", lambda e: e.memset(omlb.ap[:, 0:HR], 1.0), writes=[omlb])
    dl = A.alloc(HR, F32)
    tt("dve", dl.ap, lbT.ap[:, 0:HR], lbT.ap[:, HR:2 * HR], ALU.subtract, reads=[lbT], writes=[dl])
    act(omlb.ap[:, HR:2 * HR], dl.ap, AF.Sigmoid, reads=[dl], appends=[omlb])
    for l in range(2):
        rq = A.alloc(128, F32, parts=1)
        rk = A.alloc(128, F32, parts=1)
        mq = A.alloc(4, F32, parts=1)
        dma("sp", rq.ap, qnw_in[l:l + 1, :], writes=[rq])
        dma("sp", rk.ap, knw_in[l:l + 1, :], writes=[rk])
        tt("dve", rq.ap, rq.ap, rq.ap, ALU.mult, reads=[rq], writes=[rq])
        tt("dve", rk.ap, rk.ap, rk.ap, ALU.mult, reads=[rk], writes=[rk])
        S.op("dve", lambda e, rq=rq, mq=mq: e.tensor_reduce(out=mq.ap[:, 0:1], in_=rq.ap, axis=mybir.AxisListType.X,
                                                         op=ALU.max), reads=[rq], writes=[mq])
        S.op("dve", lambda e, rk=rk, mq=mq: e.tensor_reduce(out=mq.ap[:, 1:2], in_=rk.ap, axis=mybir.AxisListType.X,
                                                         op=ALU.max), reads=[rk, mq], writes=[mq])
        tt("dve", mq.ap[:, 2:3], mq.ap[:, 0:1], mq.ap[:, 1:2], ALU.mult, reads=[mq], writes=[mq])
        tt("dve", mq.ap[:, 3:4], mq.ap[:, 0:1], mq.ap[:, 1:2], ALU.mult, reads=[mq], writes=[mq])
        act(mq.ap[:, 2:4], mq.ap[:, 2:4], AF.Sqrt, reads=[mq], writes=[mq])
        mm(PS[1].ap[:, 0:2], consts.ap[0:1, 256:384], mq.ap[:, 2:4], True, True,
           reads=[mq, consts], writes=[PS[1]])
        ts("dve", Mb.ap[:, 2 * l:2 * l + 1], PS[1].ap[:, 0:1], math.sqrt(128.0), None, ALU.mult, reads=[PS[1]],
           appends=[Mb])
        ts("dve", Mb.ap[:, 2 * l + 1:2 * l + 2], PS[1].ap[:, 0:1], -math.sqrt(128.0), None, ALU.mult, reads=[PS[1]],
           appends=[Mb])

    chk("p0")
    kchunks = (DR + 127) // 128
    kr = min(128, DR)
    scT = A.alloc(4 * kchunks, F32)
    for kc in range(kchunks):
        dma("sp", scT.ap[0:kr, kc * 4:(kc + 1) * 4], cT_in[kc * kr:(kc + 1) * kr, :], appends=[scT])
    scS = A.alloc(4 * kchunks, F32)
    act(scS.ap[0:kr, :], scT.ap[0:kr, :], AF.Silu, reads=[scT], writes=[scS])
    pmod = A.alloc(NM, F32)
    wa = [A.alloc(kchunks * 512, F32) for _ in range(2)]
    ncb = 6 * D // 512
    it = 0
    for l in range(2):
        for cb in range(ncb):
            w = wa[it % 2]
            dma("sp", w.ap[0:kr, :].rearrange("p (k n) -> p k n", k=kchunks),
                wada_in[l, :, cb * 512:(cb + 1) * 512].rearrange("(k p) n -> p k n", p=kr), writes=[w])
            ps = PS[2 + it % 2]
            for sb in range(4):
                for kc in range(kchunks):
                    mm(ps.ap[:, sb * 4:(sb + 1) * 4], w.ap[0:kr, kc * 512 + sb * 128:kc * 512 + (sb + 1) * 128],
                       scS.ap[0:kr, kc * 4:(kc + 1) * 4], kc == 0, kc == kchunks - 1, reads=[w, scS], writes=[ps])
            off = (l * ncb + cb) * 16
            cp("dve", pmod.ap[:, off:off + 16], ps.ap[:, 0:16], reads=[ps], appends=[pmod])
            it += 1
    pmb = Buf()
    dma("pool", pm_d, pmod.ap, reads=[pmod], writes=[pmb])
    chk("p1a")
    S.op("pool", lambda e: e.collective_compute("AllGather", ALU.bypass, replica_groups=G4,
                                                ins=[pm_d], outs=[pmg_d]), reads=[pmb], writes=[pmb], kind="cc")
    pg = A.alloc(WS * NM, F32)
    dma("pool", pg.ap.rearrange("p (r n) -> p r n", r=WS), pmg_d.rearrange("(r p) n -> p r n", p=128),
        reads=[pmb], writes=[pg])
    for r in range(1, WS):
        tt("dve", pg.ap[:, 0:NM], pg.ap[:, 0:NM], pg.ap[:, r * NM:(r + 1) * NM], ALU.add, reads=[pg], writes=[pg])
    pgv = pg.ap[:, 0:NM].rearrange("p (m b) -> p m b", b=4)
    cp("dve", modT.ap, badaT.ap, reads=[badaT], writes=[modT])
    for b in range(4):
        stt(modT.ap, pgv[:, :, b], sel.ap[:, b:b + 1], modT.ap, ALU.mult, ALU.add, reads=[pg, sel, modT],
            writes=[modT])
    for l in range(2):
        for sub in range(2):
            sc_ = mod(l, 3 * sub + 1)
            g = gsc.ap[:, (l * 2 + sub) * KC:(l * 2 + sub + 1) * KC]
            nw = nwT.ap[:, (sub * 2 + l) * KC:(sub * 2 + l + 1) * KC]
            stt(g, sc_, 1.0, nw, ALU.add, ALU.mult, reads=[modT, nwT], appends=[gsc])

    chk("p1")
    wb = {}
    for l in range(2):
        for k in ("in", "out", "up", "dn"):
            src, r, cc_ = wspec[k]
            b = Buf()
            wb[(k, l)] = b
            r0 = 0
            while r0 < r:
                rr = min(128, r - r0)
                dma("pool", wsh[(k, l)][r0:r0 + rr, :], src[l, r0:r0 + rr, :], appends=[b], max_dma_last_dim=8192)
                r0 += rr
            rc = wchunk(r, cc_)
            for j in range(r // rc):
                S.op("pool", lambda e, k=k, l=l, j=j, rc=rc: e.collective_compute(
                    "AllGather", ALU.bypass, replica_groups=G4, ins=[wsh[(k, l)][j * rc:(j + 1) * rc, :]],
                    outs=[wfull[(k, l)][WS * j * rc:(WS * j + WS) * rc, :]]),
                    reads=[b], appends=[b], kind="cc")

    chk("p2")
    phase()
    xt_b = Buf()
    xin = [A.alloc(D, F32) for _ in range(2)]
    xst = [A.alloc(KC * 128, F32) for _ in range(2)]
    for tb in range(NB):
        xi, xs = xin[tb % 2], xst[tb % 2]
        dma("sp", xi.ap, x_in[tb * 128:(tb + 1) * 128, :], writes=[xi])
        for k4 in range(0, KC, 4):
            ps = PS[(k4 // 4) % 4]
            for j in range(4):
                tr(ps.ap[:, j * 128:(j + 1) * 128], xi.ap[:, (k4 + j) * 128:(k4 + j + 1) * 128], ident,
                   reads=[xi, consts], writes=[ps])
            cp("act" if (k4 // 4) % 2 else "dve", xs.ap[:, k4 * 128:(k4 + 4) * 128], ps.ap, reads=[ps], appends=[xs])
        dma("pool", XT[0][:, tb * 128:(tb + 1) * 128].rearrange("(k p) t -> p k t", p=128),
            xs.ap.rearrange("p (k t) -> p k t", k=KC), reads=[xs], appends=[xt_b])

    def prenorm_tmps(tsz, n=2):
        return dict(sq=[A.alloc(tsz, F32) for _ in range(n)], rstd=A.alloc(tsz, F32),
                    tmp=[A.alloc(tsz, F32) for _ in range(n)])

    def prenorm(l, sub, XTin, t0, hT, xr, tsz, tm, G4=None):
        g = gsc.ap[:, (l * 2 + sub) * KC:(l * 2 + sub + 1) * KC]
        sh = mod(l, 3 * sub)
        G4 = G4 or min(4, KC)
        pss = PS[6]
        sq, rstd, tmp = tm["sq"], tm["rstd"], tm["tmp"]
        nq = len(sq)
        for k4 in range(0, KC, G4):
            xb = xr[(k4 // G4) % 2]
            dma("sp", xb.ap.rearrange("p (k t) -> p k t", k=G4),
                XTin[k4 * 128:(k4 + G4) * 128, t0:t0 + tsz].rearrange("(k p) t -> p k t", p=128), writes=[xb])
            for j in range(G4):
                s_ = sq[j % nq]
                act(s_.ap, xb.ap[:, j * tsz:(j + 1) * tsz], AF.Square, reads=[xb], writes=[s_])
                mm(pss.ap[:, 0:tsz], ones, s_.ap, k4 + j == 0, k4 + j == KC - 1, reads=[s_, consts], writes=[pss])
        ts("dve", rstd.ap, pss.ap[:, 0:tsz], 1.0 / D, EPS, ALU.mult, ALU.add, reads=[pss], writes=[rstd])
        rsq(rstd)
        for k4 in range(0, KC, G4):
            xb = xr[(k4 // G4) % 2]
            dma("sp", xb.ap.rearrange("p (k t) -> p k t", k=G4),
                XTin[k4 * 128:(k4 + G4) * 128, t0:t0 + tsz].rearrange("(k p) t -> p k t", p=128), writes=[xb])
            for j in range(G4):
                kc = k4 + j
                t_ = tmp[j % nq]
                tt("dve", t_.ap, xb.ap[:, j * tsz:(j + 1) * tsz], rstd.ap, ALU.mult, reads=[xb, rstd], writes=[t_])
                act(hT.ap[:, kc * tsz:(kc + 1) * tsz], t_.ap, AF.Identity, reads=[t_, gsc, modT], appends=[hT],
                    scale=g[:, kc:kc + 1], bias=sh[:, kc:kc + 1])

    chk("pro")
    if True:
      for l in range(2):
          Xa = XT[0]
          Xb = XT[1]
          phase()
          Win = wfull[("in", l)]
          hT = A.alloc(KC * TT, BF16)
          xr = [A.alloc(min(4, KC) * TT, F32) for _ in range(2)]
          wt = [A.alloc(KC * 256, BF16) for _ in range(3)]
          ost = [A.alloc(TT, F32) for _ in range(4)]
          wfg = A.alloc(KC * HA, BF16)
          ptm = prenorm_tmps(TT)
          scr = Buf()
          O_QR, O_FR, O_IR, O_GR = 0, WR, 2 * WR, 3 * WR
          O_QA = 4 * WR
          O_KA, O_VA, O_FG = O_QA + WA, O_QA + 2 * WA, O_QA + 3 * WA
          dma("sp", wfg.ap.rearrange("p (k n) -> p k n", k=KC),
              Win[:, O_FG:O_FG + HA].rearrange("(k p) n -> p k n", p=128), reads=[wb[("in", l)]], writes=[wfg])
          wi = 0
          oi = 0
          for ttile in range(NTT):
              t0 = ttile * TT
              prenorm(l, 0, Xa, t0, hT, xr, TT, ptm)
              fm_groups = [(O_QR, WR, "q"), (O_FR, WR, "z"), (O_GR, WR, "g"), (O_QA, WA, "qa"), (O_KA, WA, "ka")]
              for (c0, width, kind) in fm_groups:
                  for cb in range(0, width, 256):
                      w = wt[wi % 3]
                      wi += 1
                      dma("sp", w.ap.rearrange("p (k n) -> p k n", k=KC),
                          Win[:, c0 + cb:c0 + cb + 256].rearrange("(k p) n -> p k n", p=128),
                          reads=[wb[("in", l)]], writes=[w])
                      for hb in range(2):
                          ps = PS[oi % 4]
                          for kc in range(KC):
                              mm(ps.ap, w.ap[:, kc * 256 + hb * 128:kc * 256 + (hb + 1) * 128],
                                 hT.ap[:, kc * TT:(kc + 1) * TT], kc == 0, kc == KC - 1, reads=[w, hT], writes=[ps])
                          o = ost[oi % 4]
                          oi += 1
                          row = cb + hb * 128
                          if kind == "q":
                              ob = o.ap.bitcast(BF16)[:, 0:TT]
                              act(ob, ps.ap, AF.Silu, reads=[ps], writes=[o])
                              dma("pool", QRT[row:row + 128, t0:t0 + TT], ob, reads=[o], appends=[scr])
                          elif kind == "g":
                              ob = o.ap.bitcast(BF16)[:, 0:TT]
                              act(ob, ps.ap, AF.Silu, reads=[ps], writes=[o])
                              dma("pool", GRT[row:row + 128, t0:t0 + TT], ob, reads=[o], appends=[scr])
                          else:
                              dst = {"z": ZRT, "qa": QAT, "ka": KAT}[kind]
                              cp("dve", o.ap, ps.ap, reads=[ps], writes=[o])
                              dma("pool", dst[row:row + 128, t0:t0 + TT], o.ap, reads=[o], appends=[scr])
              for (c0, width, dst) in [(O_IR, WR, IRd), (O_VA, WA, VAd)]:
                  for cb in range(0, width, 256):
                      w = wt[wi % 3]
                      wi += 1
                      dma("sp", w.ap.rearrange("p (k n) -> p k n", k=KC),
                          Win[:, c0 + cb:c0 + cb + 256].rearrange("(k p) n -> p k n", p=128),
                          reads=[wb[("in", l)]], writes=[w])
                      for tsb in range(TT // 128):
                          ps = PS[oi % 4]
                          for kc in range(KC):
                              mm(ps.ap[:, 0:256], hT.ap[:, kc * TT + tsb * 128:kc * TT + (tsb + 1) * 128],
                                 w.ap[:, kc * 256:(kc + 1) * 256], kc == 0, kc == KC - 1, reads=[w, hT], writes=[ps])
                          o = ost[oi % 4]
                          oi += 1
                          ob = o.ap.bitcast(BF16)[:, 0:256]
                          cp("act", ob, ps.ap[:, 0:256], reads=[ps], writes=[o])
                          dma("pool", dst[t0 + tsb * 128:t0 + (tsb + 1) * 128, cb:cb + 256], ob, reads=[o],
                              appends=[scr])
              for tsb in range(TT // 128):
                  ps = PS[oi % 4]
                  for kc in range(KC):
                      mm(ps.ap[:, 0:HA], hT.ap[:, kc * TT + tsb * 128:kc * TT + (tsb + 1) * 128],
                         wfg.ap[:, kc * HA:(kc + 1) * HA], kc == 0, kc == KC - 1, reads=[wfg, hT], writes=[ps])
                  o = ost[oi % 4]
                  oi += 1
                  cp("dve", o.ap[:, 0:HA], ps.ap[:, 0:HA], reads=[ps], writes=[o])
                  dma("pool", FGd[t0 + tsb * 128:t0 + (tsb + 1) * 128, :], o.ap[:, 0:HA], reads=[o], appends=[scr])

          chk("A%d" % l)
          phase()
          scale = 1.0 / math.sqrt(128.0)
          selb = A.alloc(16 * 128, F32, parts=16)
          dma("sp", selb.ap, selb_in, writes=[selb])
          qin = [A.alloc(TT, F32) for _ in range(2)]
          sqb = [A.alloc(TT, F32) for _ in range(2)]
          rs = [A.alloc(TT, F32) for _ in range(2)]
          qo = [A.alloc(TT, BF16) for _ in range(2)]
          i = 0
          for (src, dst, wT) in [(QAT, QNT, qnwT), (KAT, KNT, knwT)]:
              for h in range(HA):
                  for ttile in range(NTT):
                      t0 = ttile * TT
                      q_, s_, r_, o_ = qin[i % 2], sqb[i % 2], rs[i % 2], qo[i % 2]
                      ps = PS[i % 2]
                      i += 1
                      dma("sp", q_.ap, src[h * 128:(h + 1) * 128, t0:t0 + TT], writes=[q_])
                      act(s_.ap, q_.ap, AF.Square, reads=[q_], writes=[s_])
                      mm(ps.ap, ones, s_.ap, True, True, reads=[s_, consts], writes=[ps])
                      ts("dve", r_.ap, ps.ap, 1.0 / 128, EPS, ALU.mult, ALU.add, reads=[ps], writes=[r_])
                      rsq(r_)
                      tt("dve", q_.ap, q_.ap, r_.ap, ALU.mult, reads=[q_, r_], writes=[q_])
                      act(o_.ap, q_.ap, AF.Identity, reads=[q_, wT], writes=[o_], scale=wT.ap[:, l:l + 1])
                      dma("pool", dst[h * 128:(h + 1) * 128, t0:t0 + TT], o_.ap, reads=[o_], appends=[scr])
          fg = A.alloc(NB * HA, F32)
          dma("sp", fg.ap.rearrange("p (b h) -> p b h", b=NB), FGd.rearrange("(b p) h -> p b h", p=128), writes=[fg])
          fgv = fg.ap.rearrange("p (b h) -> p b h", b=NB)
          for b in range(NB):
              tt("dve", fgv[:, b, :], fgv[:, b, :], fgb.ap[:, l * HA:(l + 1) * HA], ALU.add, reads=[fg, fgb],
                 writes=[fg])
          ab = A.alloc(NB * HA, F32)
          act(ab.ap, fg.ap, AF.Abs, reads=[fg], writes=[ab])
          act(ab.ap, ab.ap, AF.Exp, reads=[ab], writes=[ab], scale=-1.0)
          act(ab.ap, ab.ap, AF.Ln, reads=[ab], writes=[ab], bias=1.0)
          lfcat = A.alloc(2 * NB * HA, F32)
          lfo = lfcat.ap[:, NB * HA:2 * NB * HA]
          ts("dve", fg.ap, fg.ap, 0.0, None, ALU.min, reads=[fg], writes=[fg])
          tt("dve", lfo, fg.ap, ab.ap, ALU.subtract, reads=[fg, ab], writes=[lfcat])
          lfb = Buf()
          dma("pool", LFd.rearrange("(b p) h -> p b h", p=128), lfo.rearrange("p (b h) -> p b h", b=NB),
              reads=[lfcat], writes=[lfb])
          S.op("pool", lambda e: e.collective_compute("AllGather", ALU.bypass,
                                                      replica_groups=[[0, 1], [2, 3], [4, 5], [6, 7]],
                                                      ins=[LFd], outs=[LF2]), reads=[lfb], writes=[lfb], kind="cc")
          dma("pool", lfcat.ap[:, 0:NB * HA].rearrange("p (b h) -> p b h", b=NB),
              LF2[0:NT, :].rearrange("(b p) h -> p b h", p=128), reads=[lfb, lfcat], writes=[lfcat])
          ts("dve", lfcat.ap[:, 0:NB * HA], lfcat.ap[:, 0:NB * HA], sel.ap[:, 5:6], None, ALU.mult, reads=[lfcat, sel],
             writes=[lfcat])
          kvb = Buf()
          for j in range(WA // RCK):
              S.op("pool", lambda e, j=j: e.collective_compute(
                  "AllGather", ALU.bypass, replica_groups=[[0, 1], [2, 3], [4, 5], [6, 7]],
                  ins=[KNT[j * RCK:(j + 1) * RCK, :]], outs=[KN2[2 * j * RCK:(2 * j + 2) * RCK, :]]),
                  reads=[scr], appends=[kvb], kind="cc")
          for j in range(NT // RCV):
              S.op("pool", lambda e, j=j: e.collective_compute(
                  "AllGather", ALU.bypass, replica_groups=[[0, 1], [2, 3], [4, 5], [6, 7]],
                  ins=[VAd[j * RCV:(j + 1) * RCV, :]], outs=[VA2[2 * j * RCV:(2 * j + 2) * RCV, :]]),
                  reads=[scr], appends=[kvb], kind="cc")
          Fc = A.alloc(2 * NB * HA, F32)
          lfv = lfcat.ap.rearrange("p (b h) -> p b h", b=2 * NB)
          for b in range(2 * NB):
              ps = PS[2 + b % 2]
              for b2 in range(b):
                  mm(ps.ap[:, 0:HA], ones, lfv[:, b2, :], b2 == 0, False, reads=[lfcat, consts], writes=[ps])
              mm(ps.ap[:, 0:HA], tri, lfv[:, b, :], b == 0, True, reads=[lfcat, consts], writes=[ps])
              cp("dve", Fc.ap[:, b * HA:(b + 1) * HA], ps.ap[:, 0:HA], reads=[ps], appends=[Fc])
          nb_ = A.alloc(2 * NB * HA, F32)
          ts("dve", nb_.ap, Fc.ap, -1.0, Mb.ap[:, 2 * l + 1:2 * l + 2], ALU.mult, ALU.add, reads=[Fc, Mb], writes=[nb_])
          ts("dve", nb_.ap[:, 0:NB * HA], nb_.ap[:, 0:NB * HA], sel.ap[:, 4:5], None, ALU.add, reads=[nb_, sel],
             writes=[nb_])
          Frow = A.alloc(NT, F32, parts=16)
          for b in range(NB):
              ps = PS[4 + b % 2]
              tr(ps.ap[0:HA, 0:128], Fc.ap[:, (NB + b) * HA:(NB + b + 1) * HA], ident, reads=[Fc, consts], writes=[ps])
              cp("dve", Frow.ap[0:HA, b * 128:(b + 1) * 128], ps.ap[0:HA, 0:128], reads=[ps], appends=[Frow])

          chk("B%d" % l)
          S.barrier()
          kTp = [A.alloc(NT, BF16) for _ in range(2)]
          kTo = [A.alloc(NT, BF16) for _ in range(2)]
          vp = [A.alloc(NB * 128, BF16) for _ in range(2)]
          vo = [A.alloc(NB * 128, BF16) for _ in range(2)]
          qn = [A.alloc(TT, BF16) for _ in range(2)]
          Fqb = [A.alloc(TT, F32) for _ in range(2)]
          stmp = [A.alloc(TT, F32) for _ in range(2)]
          PT = [A.alloc(TT, BF16) for _ in range(3)]
          rinv = A.alloc(TT, F32)
          oat = [A.alloc(TT, BF16) for _ in range(2)]
          bi = 0
          qi = 0
          for h in range(HA):
              hb = h % 2
              kj, ko = (h * 128) // RCK, (h * 128) % RCK
              dma("sp", kTp[hb].ap, KN2[2 * kj * RCK + ko:2 * kj * RCK + ko + 128, :], reads=[kvb], writes=[kTp[hb]])
              dma("sp", kTo[hb].ap, KNT[h * 128:(h + 1) * 128, :], reads=[scr], writes=[kTo[hb]])
              for vj in range(NT // RCV):
                  q_ = RCV // 128
                  dma("sp", vp[hb].ap[:, vj * RCV:(vj + 1) * RCV].rearrange("p (b d) -> p b d", b=q_),
                      VA2[2 * vj * RCV:2 * vj * RCV + RCV, h * 128:(h + 1) * 128].rearrange("(b p) d -> p b d", p=128),
                      reads=[kvb], appends=[vp[hb]])
              dma("sp", vo[hb].ap.rearrange("p (b d) -> p b d", b=NB),
                  VAd[:, h * 128:(h + 1) * 128].rearrange("(b p) d -> p b d", p=128), reads=[scr], writes=[vo[hb]])
              for j in range(NTT):
                  t0 = j * TT
                  q_ = qn[qi % 2]
                  fq = Fqb[qi % 2]
                  psO, psR = PS[4 + qi % 2 * 0], PS[5]
                  psO = PS[4]
                  qi += 1
                  dma("sp", q_.ap, QNT[h * 128:(h + 1) * 128, t0:t0 + TT], reads=[scr], writes=[q_])
                  mm(PS[6].ap, selb.ap[0:HA, h * 128:(h + 1) * 128], Frow.ap[0:HA, t0:t0 + TT], True, True,
                     reads=[selb, Frow], writes=[PS[6]])
                  cp("dve", fq.ap, PS[6].ap, reads=[PS[6]], writes=[fq])
                  blocks = [(0, kb) for kb in range(NB)] + [(1, kb) for kb in range(4 * j + 4)]
                  for bidx, (own, kb) in enumerate(blocks):
                      last = bidx == len(blocks) - 1
                      kT = (kTo if own else kTp)[hb]
                      v = (vo if own else vp)[hb]
                      dI = kb - 4 * j if own else -1
                      c0 = 128 * dI if dI > 0 else 0
                      psS = PS[bi % 2]
                      st = stmp[bi % 2]
                      p_ = PT[bi % 3]
                      bi += 1
                      mm(psS.ap[:, c0:TT], kT.ap[:, kb * 128:(kb + 1) * 128], q_.ap[:, c0:TT], True, True,
                         reads=[kT, q_], writes=[psS])
                      tt("dve", st.ap[:, c0:TT], psS.ap[:, c0:TT], fq.ap[:, c0:TT], ALU.add, reads=[psS, fq], writes=[st])
                      if dI >= 0:
                          tt("dve", st.ap[:, c0:c0 + 128], st.ap[:, c0:c0 + 128], negm.ap, ALU.add, reads=[st, negm],
                             writes=[st])
                      col = ((NB if own else 0) + kb) * HA + h
                      act(p_.ap[:, c0:TT], st.ap[:, c0:TT], AF.Exp, reads=[st, nb_], writes=[p_], scale=scale,
                          bias=nb_.ap[:, col:col + 1])
                      mm(psO.ap[:, c0:TT], v.ap[:, kb * 128:(kb + 1) * 128], p_.ap[:, c0:TT], bidx == 0, last,
                         reads=[v, p_], writes=[psO])
                      mm(psR.ap[:, c0:TT], onesb, p_.ap[:, c0:TT], bidx == 0, last, reads=[p_, cb16], writes=[psR])
                  S.op("dve", lambda e, psR=psR: e.reciprocal(out=rinv.ap, in_=psR.ap), reads=[psR], writes=[rinv])
                  o_ = oat[qi % 2]
                  tt("dve", o_.ap, psO.ap, rinv.ap, ALU.mult, reads=[psO, rinv], writes=[o_])
                  dma("pool", MIXT[WR + h * 128:WR + (h + 1) * 128, t0:t0 + TT], o_.ap, reads=[o_], appends=[scr])

          chk("C%d" % l)
          phase()
          NG = NT // 512
          CS = A.alloc(HR * NCH * 3, F32)
          csv = CS.ap.rearrange("p (h c k) -> p h c k", h=HR, c=NCH)
          zt = [A.alloc(512, F32) for _ in range(2)]
          qs = [A.alloc(512, BF16) for _ in range(2)]
          kk = [A.alloc(512, F32) for _ in range(2)]
          lf = [A.alloc(512, F32) for _ in range(2)]
          Gt = [A.alloc(512, F32) for _ in range(2)]
          E1 = [A.alloc(512, F32) for _ in range(2)]
          qg = [A.alloc(512, BF16) for _ in range(2)]
          kg = [A.alloc(512, BF16) for _ in range(2)]
          dlrs = [A.alloc(NC5, F32) for _ in range(2)]
          i = 0
          for h in range(HR):
              for g in range(NG):
                  t0 = g * 512
                  z_, q_, k_, l_, G_, E_, qg_, kg_ = (zt[i % 2], qs[i % 2], kk[i % 2], lf[i % 2], Gt[i % 2], E1[i % 2],
                                                      qg[i % 2], kg[i % 2])
                  i += 1
                  dma("sp", z_.ap, ZRT[h * 128:(h + 1) * 128, t0:t0 + 512], reads=[scr], writes=[z_])
                  dma("sp", q_.ap, QRT[h * 128:(h + 1) * 128, t0:t0 + 512], reads=[scr], writes=[q_])
                  act(z_.ap, z_.ap, AF.Sigmoid, reads=[z_], writes=[z_], scale=-1.0)
                  ts("dve", k_.ap, z_.ap, omlb.ap[:, l * HR + h:l * HR + h + 1], K_MAX, ALU.mult, ALU.min,
                     reads=[z_, omlb], writes=[k_])
                  act(l_.ap, k_.ap, AF.Ln, reads=[k_], writes=[l_], scale=-1.0, bias=1.0)
                  for cchunk in range(NC5):
                      sl = slice(cchunk * CH, (cchunk + 1) * CH)
                      S.op("dve", lambda e, G_=G_, l_=l_, sl=sl: e.tensor_tensor_scan(
                          out=G_.ap[:, sl], data0=ones[:, 0:CH], data1=l_.ap[:, sl], initial=0.0, op0=ALU.mult,
                          op1=ALU.add), reads=[l_, consts], writes=[G_])
                  Gv = G_.ap.rearrange("p (c t) -> p c t", t=CH)
                  cidx = g * NC5
                  act(csv[:, h, cidx:cidx + NC5, 0], Gv[:, :, CH - 1], AF.Exp, reads=[G_], appends=[CS])
                  act(csv[:, h, cidx:cidx + NC5, 2], Gv[:, :, CH // 2 - 1], AF.Exp, reads=[G_], appends=[CS])
                  dlr = dlrs[i % 2]
                  tt("dve", dlr.ap, Gv[:, :, CH - 1], Gv[:, :, CH // 2 - 1], ALU.subtract, reads=[G_], writes=[dlr])
                  act(csv[:, h, cidx:cidx + NC5, 1], dlr.ap, AF.Exp, reads=[dlr], appends=[CS])
                  for cchunk in range(NC5):
                      sl = slice(cchunk * CH, (cchunk + 1) * CH)
                      ts("dve", l_.ap[:, sl], G_.ap[:, sl], Gv[:, cchunk, CH // 2 - 1:CH // 2], None, ALU.subtract, reads=[G_, l_],
                         writes=[l_])
                  act(E_.ap, l_.ap, AF.Exp, reads=[l_], writes=[E_])
                  tt("dve", qg_.ap, q_.ap, E_.ap, ALU.mult, reads=[q_, E_], writes=[qg_])
                  act(E_.ap, l_.ap, AF.Exp, reads=[l_, qg_], writes=[E_], scale=-1.0)
                  tt("dve", kg_.ap, k_.ap, E_.ap, ALU.mult, reads=[k_, E_], writes=[kg_])
                  dma("pool", QGT[h * 128:(h + 1) * 128, t0:t0 + 512], qg_.ap, reads=[qg_], appends=[scr])
                  dma("pool", KGT[h * 128:(h + 1) * 128, t0:t0 + 512], kg_.ap, reads=[kg_], appends=[scr])

          chk("D1%d" % l)
          S.barrier()
          SSall = A.alloc(HR * 128, F32)
          SSh = [Tile(SSall.ap[:, h * 128:(h + 1) * 128]) for h in range(HR)]
          PSBv = [Tile(PSB.ap[:, 0:128]), Tile(PSB.ap[:, 128:256])]
          S.op("dve", lambda e: e.memset(SSall.ap, 0.0), writes=[SSall] + SSh)
          kgl = [A.alloc(HR * GS, BF16) for _ in range(2)]
          qgl = [A.alloc(HR * GS, BF16) for _ in range(2)]
          vl = [A.alloc(NCG * WR, BF16, parts=CH) for _ in range(2)]
          kgtm = [A.alloc(128, BF16, parts=CH) for _ in range(4)]
          utmp = [A.alloc(128, F32) for _ in range(4)]
          Sb = [A.alloc(128, BF16) for _ in range(4)]
          ATm = [A.alloc(CH, BF16, parts=CH) for _ in range(4)]
          OTs = A.alloc(HR * GS, F32)
          sqo = [A.alloc(GS, F32) for _ in range(2)]
          rso = [A.alloc(GS, F32) for _ in range(2)]
          grl = [A.alloc(GS, BF16) for _ in range(2)]
          yo = [A.alloc(GS, BF16) for _ in range(2)]
          sendb = Buf()
          ui = 0
          for pss_ in range(2):
              emit = pss_ == 1
              for g in range(NT // GS):
                  t0 = g * GS
                  kgl_, qgl_, vl_ = kgl[g % 2], qgl[g % 2], vl[g % 2]
                  dma("sp", kgl_.ap.rearrange("p (h t) -> p h t", h=HR),
                      KGT[:, t0:t0 + GS].rearrange("(h p) t -> p h t", p=128), reads=[scr], writes=[kgl_])
                  dma("sp", vl_.ap.rearrange("p (c w) -> p c w", c=NCG),
                      IRd[t0:t0 + GS, :].rearrange("(c p) w -> p c w", p=CH), reads=[scr], writes=[vl_])
                  if emit:
                      dma("sp", qgl_.ap.rearrange("p (h t) -> p h t", h=HR),
                          QGT[:, t0:t0 + GS].rearrange("(h p) t -> p h t", p=128), reads=[scr], writes=[qgl_])
                  for cchunk in range(NCG):
                      cg = g * NCG + cchunk
                      for h in range(HR):
                          u = ui % 4
                          ui += 1
                          kgc = kgl_.ap[:, h * GS + cchunk * CH:h * GS + (cchunk + 1) * CH]
                          vc = vl_.ap[:, cchunk * WR + h * 128:cchunk * WR + (h + 1) * 128]
                          Sh = SSh[h].ap
                          SS = SSh[h]
                          if emit:
                              qgc = qgl_.ap[:, h * GS + cchunk * CH:h * GS + (cchunk + 1) * CH]
                              act(Sb[u].ap, Sh, AF.Identity, reads=[SS, CS], writes=[Sb[u]], scale=csv[:, h, cg, 2:3])
                              psA = PS[u % 2]
                              mm(psA.ap[0:CH, 0:CH], kgc, qgc, True, True, reads=[kgl_, qgl_], writes=[psA])
                              tt("dve", ATm[u].ap, psA.ap[0:CH, 0:CH], tri[0:CH, 0:CH], ALU.mult, reads=[psA, consts],
                                 writes=[ATm[u]])
                              psO = PS[2 + u % 2]
                              mm(psO.ap[:, 0:CH], vc, ATm[u].ap, True, False, reads=[vl_, ATm[u]], writes=[psO])
                              mm(psO.ap[:, 0:CH], Sb[u].ap, qgc, False, True, reads=[Sb[u], qgl_], writes=[psO])
                              cp("act", OTs.ap[:, h * GS + cchunk * CH:h * GS + (cchunk + 1) * CH], psO.ap[:, 0:CH],
                                 reads=[psO], appends=[OTs])
                          tr(PSBv[u % 2].ap[0:CH, :], kgc, identb, reads=[kgl_, cb16],
                             writes=[PSBv[u % 2]])
                          cp("act", kgtm[u].ap, PSBv[u % 2].ap[0:CH, :], reads=[PSBv[u % 2]],
                             writes=[kgtm[u]])
                          psU = PS[4 + u % 2]
                          mm(psU.ap[:, 0:128], kgtm[u].ap, vc, True, True, reads=[kgtm[u], vl_], writes=[psU])
                          ts("dve", utmp[u].ap, psU.ap[:, 0:128], csv[:, h, cg, 1:2], None, ALU.mult, reads=[psU, CS],
                             writes=[utmp[u]])
                          stt(Sh, Sh, csv[:, h, cg, 0:1], utmp[u].ap, ALU.mult, ALU.add, reads=[SS, utmp[u], CS],
                              writes=[SS])
                  if emit:
                      for h in range(HR):
                          s_, r_, g_, y_ = sqo[h % 2], rso[h % 2], grl[h % 2], yo[h % 2]
                          oh = OTs.ap[:, h * GS:(h + 1) * GS]
                          dma("sp", g_.ap, GRT[h * 128:(h + 1) * 128, t0:t0 + GS], reads=[scr], writes=[g_])
                          act(s_.ap, oh, AF.Square, reads=[OTs], writes=[s_])
                          mm(PS[6].ap[:, 0:GS], ones, s_.ap, True, True, reads=[s_, consts], writes=[PS[6]])
                          ts("dve", r_.ap, PS[6].ap[:, 0:GS], 1.0 / 128, EPS, ALU.mult, ALU.add, reads=[PS[6]], writes=[r_])
                          rsq(r_)
                          tt("dve", r_.ap, r_.ap, oh, ALU.mult, reads=[r_, OTs], writes=[r_])
                          stt(y_.ap, r_.ap, rnwT.ap[:, l:l + 1], g_.ap, ALU.mult, ALU.mult, reads=[r_, g_, rnwT],
                              writes=[y_])
                          dma("pool", MIXT[h * 128:(h + 1) * 128, t0:t0 + GS], y_.ap, reads=[y_], appends=[scr])
              if not emit:
                  dma("pool", SEND.rearrange("(h p) v -> p h v", p=128), SSall.ap.rearrange("p (h v) -> p h v", h=HR),
                      reads=SSh, writes=[sendb])
                  S.op("pool", lambda e: e.collective_compute("AllGather", ALU.bypass,
                                                              replica_groups=[[0, 1], [2, 3], [4, 5], [6, 7]],
                                                              ins=[SEND], outs=[SGAT]), reads=[sendb], writes=[sendb],
                       kind="cc")
                  dma("pool", SSall.ap.rearrange("p (h v) -> p h v", h=HR),
                      SGAT[0:WR, :].rearrange("(h p) v -> p h v", p=128), reads=[sendb], writes=SSh)
                  ts("dve", SSall.ap, SSall.ap, sel.ap[:, 5:6], None, ALU.mult, reads=SSh + [sel], writes=SSh)

          chk("D%d" % l)
          phase()
          Wout = wfull[("out", l)]
          mT = A.alloc(KM * TT, BF16)
          wt = [A.alloc(KM * 256, BF16) for _ in range(3)]
          xres = [A.alloc(TT, F32) for _ in range(3)]
          xo = [A.alloc(TT, F32) for _ in range(3)]
          wi = 0
          oi = 0
          g1 = mod(l, 2)
          for ttile in range(NTT):
              t0 = ttile * TT
              dma("sp", mT.ap.rearrange("p (k t) -> p k t", k=KM),
                  MIXT[:, t0:t0 + TT].rearrange("(k p) t -> p k t", p=128), reads=[scr], writes=[mT])
              for cb in range(0, D, 256):
                  w = wt[wi % 3]
                  wi += 1
                  dma("sp", w.ap.rearrange("p (k n) -> p k n", k=KM),
                      Wout[:, cb:cb + 256].rearrange("(k p) n -> p k n", p=128), reads=[wb[("out", l)]], writes=[w])
                  for hb in range(2):
                      nb0 = (cb + hb * 128) // 128
                      ps = PS[oi % 4]
                      xr_, xo_ = xres[oi % 3], xo[oi % 3]
                      oi += 1
                      dma("sp", xr_.ap, Xa[nb0 * 128:(nb0 + 1) * 128, t0:t0 + TT], reads=[xt_b], writes=[xr_])
                      for kc in range(KM):
                          mm(ps.ap, w.ap[:, kc * 256 + hb * 128:kc * 256 + (hb + 1) * 128],
                             mT.ap[:, kc * TT:(kc + 1) * TT], kc == 0, kc == KM - 1, reads=[w, mT], writes=[ps])
                      stt(xo_.ap, ps.ap, g1[:, nb0:nb0 + 1], xr_.ap, ALU.mult, ALU.add, reads=[ps, xr_, modT],
                          writes=[xo_])
                      dma("pool", Xb[nb0 * 128:(nb0 + 1) * 128, t0:t0 + TT], xo_.ap, reads=[xo_], appends=[xt_b])

          chk("E%d" % l)
          phase()
          Wup, Wdn = wfull[("up", l)], wfull[("dn", l)]
          hreg = A.alloc(max(KC * TF // 2, 5 * TF), F32)
          hT = Tile(hreg.ap.bitcast(BF16)[:, 0:KC * TF])
          al = [hreg.ap[:, i * TF:(i + 1) * TF] for i in range(5)]
          uT = A.alloc(FC * TF, BF16)
          xr = [A.alloc(TF, F32) for _ in range(2)]
          GF = min(8, FC)
          UW = 128
          wt = [A.alloc(max(KC * UW, GF * 512), BF16) for _ in range(3)]
          rtmp = [A.alloc(TF, F32) for _ in range(1)]
          g2 = mod(l, 5)
          ptm = prenorm_tmps(TF, 1)
          wi = 0
          oi = 0
          for ttile in range(NT // TF):
              t0 = ttile * TF
              prenorm(l, 1, Xb, t0, hT, xr, TF, ptm, G4=1)
              for cb in range(0, DFF, UW):
                  w = wt[wi % 3]
                  wi += 1
                  dma("sp", w.ap[:, 0:KC * UW].rearrange("p (k n) -> p k n", k=KC),
                      Wup[:, cb:cb + UW].rearrange("(k p) n -> p k n", p=128), reads=[wb[("up", l)]], writes=[w])
                  fc = cb // 128
                  ps = PS[oi % 2]
                  r_ = rtmp[0]
                  oi += 1
                  for kc in range(KC):
                      mm(ps.ap[:, 0:TF], w.ap[:, kc * UW:(kc + 1) * UW],
                         hT.ap[:, kc * TF:(kc + 1) * TF], kc == 0, kc == KC - 1, reads=[w, hT], writes=[ps])
                  act(r_.ap, ps.ap[:, 0:TF], AF.Relu, reads=[ps], writes=[r_])
                  tt("dve", uT.ap[:, fc * TF:(fc + 1) * TF], r_.ap, r_.ap, ALU.mult, reads=[r_], appends=[uT])
              for n0 in range(0, D, 512):
                  for fg_ in range(0, FC, GF):
                      w = wt[wi % 3]
                      wi += 1
                      dma("sp", w.ap[:, 0:GF * 512].rearrange("p (k n) -> p k n", k=GF),
                          Wdn[fg_ * 128:(fg_ + GF) * 128, n0:n0 + 512].rearrange("(k p) n -> p k n", p=128),
                          reads=[wb[("dn", l)]], writes=[w])
                      for f in range(GF):
                          fc = fg_ + f
                          for nb in range(4):
                              mm(PS[2 + nb].ap[:, 0:TF], w.ap[:, f * 512 + nb * 128:f * 512 + (nb + 1) * 128],
                                 uT.ap[:, fc * TF:(fc + 1) * TF], fc == 0, fc == FC - 1, reads=[w, uT],
                                 writes=[PS[2 + nb]])
                  for nb in range(4):
                      nb0 = n0 // 128 + nb
                      xr_, xo_ = al[nb % 2], al[2 + nb % 2]
                      dma("sp", xr_, Xb[nb0 * 128:(nb0 + 1) * 128, t0:t0 + TF], reads=[xt_b], appends=[hT])
                      stt(xo_, PS[2 + nb].ap[:, 0:TF], g2[:, nb0:nb0 + 1], xr_, ALU.mult, ALU.add,
                          reads=[PS[2 + nb], hT, modT], appends=[hT])
                      if l == 0:
                          dma("pool", Xa[nb0 * 128:(nb0 + 1) * 128, t0:t0 + TF], xo_, reads=[hT], appends=[xt_b])
                      else:
                          ys = al[4]
                          for tsb in range(TF // 128):
                              tr(PS[6].ap[:, tsb * 128:(tsb + 1) * 128], xo_[:, tsb * 128:(tsb + 1) * 128], ident,
                                 reads=[hT, consts], writes=[PS[6]])
                          cp("act", ys, PS[6].ap[:, 0:TF], reads=[PS[6]], appends=[hT])
                          dma("pool", y_out[t0:t0 + TF, nb0 * 128:(nb0 + 1) * 128].rearrange("(s p) n -> p s n", p=128),
                              ys.rearrange("p (s n) -> p s n", s=TF // 128), reads=[hT], appends=[xt_b])
    S.barrier()
    S.emit(nc, stack)
    stack.close()
    return nc


def make_consts():
    c = np.zeros((128, 384), np.float32)
    c[:, 0:128] = np.eye(128, dtype=np.float32)
    c[:, 128:256] = np.triu(np.ones((128, 128), np.float32))
    c[:, 256:384] = 1.0
    sb = np.zeros((16, 16, 128), np.float32)
    for h in range(16):
        sb[h, h, :] = math.sqrt(128.0)
    return c, sb.reshape(16, 2048)


def make_in_maps(cfg, x, c, lower_bounds, w_ada, b_ada, norm_mix_w, norm_ffn_w, w_in, rec_norm_w, fg_bias,
                 q_norm_w, k_norm_w, w_out, w_up, w_down):
    f = lambda a: np.ascontiguousarray(np.asarray(a, dtype=np.float32))
    D, NT, KC, DR = cfg.D, cfg.NT, cfg.KC, cfg.DR
    consts, selb = make_consts()
    x, c, w_ada, w_in, w_out, w_up, w_down = f(x), f(c), f(w_ada), f(w_in), f(w_out), f(w_up), f(w_down)
    cT = np.ascontiguousarray(c.T)
    maps = []
    AGLIM = cfg.AGLIM

    def p2floor(v):
        p = 1
        while p * 2 <= v:
            p *= 2
        return p

    def bc(w, R, r):
        L, rows, C = w.shape
        rc = min(R, p2floor(AGLIM // (cfg.WS * C * 2)))
        return np.ascontiguousarray(w.reshape(L, rows // (cfg.WS * rc), cfg.WS, rc, C)[:, :, r].reshape(L, R, C))

    for core in range(NCORES):
        b, s = core // 2, core % 2
        sel = np.zeros((128, 8), np.float32)
        sel[:, b] = 1.0
        sel[:, 4] = 0.0 if s == 1 else -1e30
        sel[:, 5] = 1.0 if s == 1 else 0.0
        r = core % cfg.WS
        mo = cfg.MIXW // cfg.WS
        fo = cfg.DFF // cfg.WS
        maps.append({
            "x": f(x[b, s * NT:(s + 1) * NT, :]),
            "cT": f(cT[r * DR:(r + 1) * DR, :]),
            "w_ada": f(w_ada[:, r * DR:(r + 1) * DR, :]),
            "b_ada": f(b_ada).reshape(12 * KC, 128),
            "norm_mix_w": f(norm_mix_w).reshape(2 * KC, 128),
            "norm_ffn_w": f(norm_ffn_w).reshape(2 * KC, 128),
            "lower_bounds": f(lower_bounds).reshape(2 * cfg.HR, 128),
            "rec_norm_w": f(rec_norm_w), "q_norm_w": f(q_norm_w), "k_norm_w": f(k_norm_w),
            "fg_bias": f(fg_bias).reshape(1, 2 * cfg.HA),
            "w_in": bc(w_in, DR, r), "w_out": bc(w_out, mo, r), "w_up": bc(w_up, DR, r), "w_down": bc(w_down, fo, r),
            "consts": consts, "selb": selb, "sel": sel,
        })
    return maps


def run(cfg, **inputs):
    nc = build(cfg)
    maps = make_in_maps(cfg, **inputs)
    res = run_bass_kernel_spmd(nc, maps, core_ids=list(range(NCORES)))
    out = np.zeros((cfg.B, cfg.SEQ, cfg.D), np.float32)
    for core in range(NCORES):
        b, s = core // 2, core % 2
        out[b, s * cfg.NT:(s + 1) * cfg.NT, :] = res.results[core]["y"]
    return out


def kernel(**inputs):
    return run(Cfg(), **inputs)
```
